# Optimizing a Trainium2 kernel written in Bass

```python
import jax
import jax.numpy as jnp
from jax import lax
import numpy as np

D_MODEL = 1024
BATCH = 2
SEQ = 16384
DEPTH = 1
DEC_BATCH = 8
DEC_SEQ = 8192
PAST_LEN = 128

GRID_W = 64
QBLK = 128
HD = 64
N_HEADS_A = 8
N_KV_A = 2
GROUP_A = N_HEADS_A // N_KV_A
N_HEADS_B = 8
N_HEADS_M = 4
HD_M = 128
N_MEM = 256
WIN_R_MAX = 8
WIN_C = 16
BRANCH_W = 512
N_BRANCH = 3
ROPE_THETA = 10000.0
ROPE_FREQS = HD // 4
EPS_QK = 1e-6
EPS_LN = 1e-5
DN_ALPHA = (2 * DEPTH) ** 0.25
DN_BETA = (8 * DEPTH) ** -0.25
NEG_INF = -1e30
SPLIT_SIZES = (N_HEADS_A * HD, N_KV_A * HD, N_KV_A * HD, BRANCH_W,
               N_HEADS_B * HD, N_HEADS_B * HD, N_HEADS_B * HD, BRANCH_W,
               N_HEADS_M * HD_M, BRANCH_W, N_BRANCH * D_MODEL)
SPLIT_POINTS = tuple(sum(SPLIT_SIZES[:i + 1]) for i in range(len(SPLIT_SIZES) - 1))
D_IN = sum(SPLIT_SIZES)

kernel_name = 'hybrid_gqa_natten_mem_encoder'


def _rms_norm(x, g):
    xf = x.astype(jnp.float32)
    y = xf * lax.rsqrt(jnp.mean(xf * xf, axis=-1, keepdims=True) + EPS_QK)
    return (y * g.astype(jnp.float32)).astype(x.dtype)


def _layer_norm(x, g, b):
    xf = x.astype(jnp.float32)
    mu = jnp.mean(xf, axis=-1, keepdims=True)
    var = jnp.mean(jnp.square(xf - mu), axis=-1, keepdims=True)
    y = (xf - mu) * lax.rsqrt(var + EPS_LN) * g.astype(jnp.float32) + b.astype(jnp.float32)
    return y.astype(x.dtype)


def _axial_rope_tables(n_tok):
    t = jnp.arange(n_tok)
    pos = jnp.stack([t // GRID_W, t % GRID_W], axis=-1).astype(jnp.float32)
    inv_freq = ROPE_THETA ** (-jnp.arange(ROPE_FREQS, dtype=jnp.float32) / ROPE_FREQS)
    ang = pos[:, :, None] * inv_freq
    return jnp.cos(ang), jnp.sin(ang)


def _apply_axial_rope(x, cos, sin):
    b, s, h, d = x.shape
    xf = x.astype(jnp.float32).reshape(b, s, h, 2, 2, ROPE_FREQS)
    x1, x2 = xf[..., 0, :], xf[..., 1, :]
    c = cos[None, :, None]
    sn = sin[None, :, None]
    out = jnp.stack([x1 * c - x2 * sn, x2 * c + x1 * sn], axis=-2)
    return out.reshape(b, s, h, d).astype(x.dtype)


def _global_gqa(q, k, v):
    b, s, _, _ = q.shape
    nb = s // QBLK
    qb = q.reshape(b, nb, QBLK, N_KV_A, GROUP_A, HD).transpose(1, 0, 2, 3, 4, 5)
    scale = HD ** -0.5

    def one_block(qi):
        sc = jnp.einsum('bqkgd,bskd->bkgqs', qi, k, preferred_element_type=jnp.float32) * scale
        p = jax.nn.softmax(sc, axis=-1).astype(v.dtype)
        return jnp.einsum('bkgqs,bskd->bqkgd', p, v)

    out = lax.map(one_block, qb)
    return out.transpose(1, 0, 2, 3, 4, 5).reshape(b, s, N_HEADS_A * HD)


def _neighbourhood_attn(q, k, v, rpb):
    b, s, h, d = q.shape
    rows = s // GRID_W
    win_r = min(WIN_R_MAX, rows)
    reg_r = min(win_r + 1, rows)
    rows_per_blk = QBLK // GRID_W
    nb = s // QBLK
    n_reg = reg_r * GRID_W
    kg = k.reshape(b, rows, GRID_W, h, d)
    vg = v.reshape(b, rows, GRID_W, h, d)
    qb = q.reshape(b, nb, QBLK, h, d).transpose(1, 0, 2, 3, 4)
    q_loc = jnp.arange(QBLK)
    k_loc = jnp.arange(n_reg)
    q_col = q_loc % GRID_W
    k_col = k_loc % GRID_W
    q_wc = jnp.clip(q_col - WIN_C // 2, 0, GRID_W - WIN_C)
    col_ok = (k_col[None] >= q_wc[:, None]) & (k_col[None] < q_wc[:, None] + WIN_C)
    dc = jnp.clip(k_col[None] - q_col[:, None] + WIN_C - 1, 0, 2 * WIN_C - 2)
    scale = d ** -0.5

    def one_block(args):
        i, qi = args
        r0 = i * rows_per_blk
        rs = jnp.clip(r0 - win_r // 2, 0, rows - reg_r)
        kb = lax.dynamic_slice_in_dim(kg, rs, reg_r, axis=1).reshape(b, n_reg, h, d)
        vb = lax.dynamic_slice_in_dim(vg, rs, reg_r, axis=1).reshape(b, n_reg, h, d)
        q_row = r0 + q_loc // GRID_W
        k_row = rs + k_loc // GRID_W
        q_wr = jnp.clip(q_row - win_r // 2, 0, rows - win_r)
        row_ok = (k_row[None] >= q_wr[:, None]) & (k_row[None] < q_wr[:, None] + win_r)
        dr = jnp.clip(k_row[None] - q_row[:, None] + WIN_R_MAX - 1, 0, 2 * WIN_R_MAX - 2)
        bias = rpb[:, dr, dc].astype(jnp.float32)
        sc = jnp.einsum('bqhd,bkhd->bhqk', qi, kb, preferred_element_type=jnp.float32) * scale + bias[None]
        sc = jnp.where((row_ok & col_ok)[None, None], sc, NEG_INF)
        p = jax.nn.softmax(sc, axis=-1).astype(vb.dtype)
        return jnp.einsum('bhqk,bkhd->bqhd', p, vb)

    out = lax.map(one_block, (jnp.arange(nb), qb))
    return out.transpose(1, 0, 2, 3, 4).reshape(b, s, h * d)


def _memory_attn(q, k, v):
    b, s, h, d = q.shape
    sc = jnp.einsum('bqhd,bmhd->bhqm', q, k, preferred_element_type=jnp.float32) * (d ** -0.5)
    p = jax.nn.softmax(sc, axis=-1).astype(v.dtype)
    return jnp.einsum('bhqm,bmhd->bqhd', p, v).reshape(b, s, h * d)


def _layer(x, mem, w_in, q_norm, k_norm, rpb, w_mem_kv, w_branch, w_out, ln_g, ln_b):
    b, s, _ = x.shape
    hproj = x @ w_in
    qa, ka, va, za, qb, kb, vb, zb, qm, zm, gl = jnp.split(hproj, SPLIT_POINTS, axis=-1)
    cos, sin = _axial_rope_tables(s)
    qa = _apply_axial_rope(_rms_norm(qa.reshape(b, s, N_HEADS_A, HD), q_norm), cos, sin)
    ka = _apply_axial_rope(_rms_norm(ka.reshape(b, s, N_KV_A, HD), k_norm), cos, sin)
    oa = _global_gqa(qa, ka, va.reshape(b, s, N_KV_A, HD))
    ob = _neighbourhood_attn(qb.reshape(b, s, N_HEADS_B, HD), kb.reshape(b, s, N_HEADS_B, HD),
                             vb.reshape(b, s, N_HEADS_B, HD), rpb)
    km, vm = jnp.split(mem @ w_mem_kv, 2, axis=-1)
    om = _memory_attn(qm.reshape(b, s, N_HEADS_M, HD_M), km.reshape(b, -1, N_HEADS_M, HD_M),
                      vm.reshape(b, -1, N_HEADS_M, HD_M))
    branches = jnp.stack([oa * jax.nn.silu(za), ob * jax.nn.silu(zb), om * jax.nn.silu(zm)], axis=2)
    proj = jnp.einsum('bsnc,ncd->bsnd', branches, w_branch)
    gates = jax.nn.sigmoid(gl.reshape(b, s, N_BRANCH, D_MODEL))
    merged = jnp.sum(gates * proj, axis=2)
    out = merged @ w_out
    return _layer_norm(DN_ALPHA * x + out, ln_g, ln_b)


def setup_inputs(seed: int = 0) -> dict:
    key = jax.random.key(seed)
    ks = jax.random.split(key, 13)
    f32 = jnp.float32
    x_prompt = jax.random.normal(ks[0], (BATCH, SEQ, D_MODEL), f32)
    x_sample = jax.random.normal(ks[1], (DEC_BATCH, DEC_SEQ, D_MODEL), f32)
    mem_prompt = jax.random.normal(ks[2], (BATCH, N_MEM, D_MODEL), f32)
    mem_sample = jax.random.normal(ks[3], (DEC_BATCH, N_MEM, D_MODEL), f32)
    w_in = jax.random.normal(ks[4], (DEPTH, D_MODEL, D_IN), f32) * D_MODEL ** -0.5
    q_norm = 1.0 + 0.01 * jax.random.normal(ks[5], (DEPTH, HD), f32)
    k_norm = 1.0 + 0.01 * jax.random.normal(ks[6], (DEPTH, HD), f32)
    rpb = 0.02 * jax.random.normal(ks[7], (DEPTH, N_HEADS_B, 2 * WIN_R_MAX - 1, 2 * WIN_C - 1), f32)
    w_mem_kv = jax.random.normal(ks[8], (DEPTH, D_MODEL, 2 * N_HEADS_M * HD_M), f32) * D_MODEL ** -0.5
    w_branch = jax.random.normal(ks[9], (DEPTH, N_BRANCH, BRANCH_W, D_MODEL), f32) * (BRANCH_W ** -0.5 * DN_BETA)
    w_out = jax.random.normal(ks[10], (DEPTH, D_MODEL, D_MODEL), f32) * (D_MODEL ** -0.5 * DN_BETA)
    ln_g = 1.0 + 0.01 * jax.random.normal(ks[11], (DEPTH, D_MODEL), f32)
    ln_b = 0.01 * jax.random.normal(ks[12], (DEPTH, D_MODEL), f32)
    return {'x_prompt': x_prompt, 'x_sample': x_sample, 'mem_prompt': mem_prompt, 'mem_sample': mem_sample,
            'w_in': w_in, 'q_norm': q_norm, 'k_norm': k_norm, 'rpb': rpb, 'w_mem_kv': w_mem_kv,
            'w_branch': w_branch, 'w_out': w_out, 'ln_g': ln_g, 'ln_b': ln_b}


def reference(x_prompt, x_sample, mem_prompt, mem_sample, w_in, q_norm, k_norm, rpb, w_mem_kv,
              w_branch, w_out, ln_g, ln_b):
    y_prompt = x_prompt
    y_sample = x_sample
    for l in range(DEPTH):
        y_prompt = _layer(y_prompt, mem_prompt, w_in[l], q_norm[l], k_norm[l], rpb[l], w_mem_kv[l],
                          w_branch[l], w_out[l], ln_g[l], ln_b[l])
        y_sample = _layer(y_sample, mem_sample, w_in[l], q_norm[l], k_norm[l], rpb[l], w_mem_kv[l],
                          w_branch[l], w_out[l], ln_g[l], ln_b[l])
    return (y_prompt, y_sample)
```

```python
import types
import collections
import numpy as np
from contextlib import ExitStack
import concourse.bass as bass
import concourse.mybir as mybir
from concourse.bass_utils import run_bass_kernel_spmd

F32 = mybir.dt.float32
BF16 = mybir.dt.bfloat16
ALU = mybir.AluOpType
AF = mybir.ActivationFunctionType
AX = mybir.AxisListType

D = 1024
T = 256
NSLOT = 2
NPB = 3
NPAR = 4
NT = 29
TILE_EL = [4096] * 8 + [4096, 4096, 4096, 4096, 4096, 2048] * 2 + [16] * 4 + [4096, 4096] + [2048, 4096, 4096]
GRID_W = 64
DN_ALPHA = 2.0 ** 0.25
DN_BETA = 8.0 ** -0.25
EPS_QK = 1e-6
EPS_LN = 1e-5
OFFS = {0: [-2, -1, 0, 1, 2, 3], 1: [-2, -1, 0, 1, 2], 2: [-2, -1, 0, 1, 2], 3: [-2, -1, 0, 1, 2],
        4: [-3, -2, -1, 0, 1, 2]}


def blk_class(i, nl):
    if i == 0:
        return 0
    if i == 1:
        return 1
    if i == nl - 1:
        return 4
    if i == nl - 2:
        return 3
    return 2


def _freeze(fn):
    if fn.__closure__ is None:
        return fn
    cells = []
    for c in fn.__closure__:
        try:
            cells.append(types.CellType(c.cell_contents))
        except ValueError:
            cells.append(c)
    g = types.FunctionType(fn.__code__, fn.__globals__, fn.__name__, fn.__defaults__, tuple(cells))
    g.__kwdefaults__ = fn.__kwdefaults__
    return g


class _StopEmission(Exception):
    pass


DEBUG_STOP = [None]


def _phase(name):
    if DEBUG_STOP[0] is not None and name == DEBUG_STOP[0]:
        raise _StopEmission()


class Prog:
    def __init__(self):
        self.engs = {'pe': [], 'act': [], 'dve': [], 'pool': [], 'sp': []}
        self.cnt = {}
        self.waited = {e: {} for e in self.engs}
        self.lw = {}
        self.lr = {}
        self.const = set()
        self.n_inst = 0

    def _deps(self, reads, writes):
        d = {}

        def add(t):
            if t is None:
                return
            k, v = t
            if d.get(k, 0) < v:
                d[k] = v
        for k in reads:
            add(self.lw.get(k))
        for k in writes:
            add(self.lw.get(k))
            for kk, vv in self.lr.get(k, {}).items():
                add((kk, vv))
        return d

    def _filter(self, eng, d):
        out = []
        w = self.waited[eng]
        for k, v in d.items():
            if w.get(k, 0) < v:
                w[k] = v
                out.append((k, v))
        return out

    def _commit(self, ticket, reads, writes):
        k0, v0 = ticket
        for k in reads:
            if k in self.const:
                continue
            r = self.lr.setdefault(k, {})
            if r.get(k0, 0) < v0:
                r[k0] = v0
        for k in writes:
            self.lw[k] = ticket
            self.lr[k] = {}

    def emit(self, eng, fns, reads=(), writes=()):
        if callable(fns):
            fns = [fns]
        fns = [_freeze(f) for f in fns]
        d = self._deps(reads, writes)
        if eng == 'pe':
            d.pop(('eng', 'pe'), None)
        waits = self._filter(eng, d)
        key = ('eng', eng)
        self.cnt[key] = self.cnt.get(key, 0) + 1
        ticket = (key, self.cnt[key])
        self.engs[eng].append((waits, fns, key, 1))
        self.waited[eng][key] = max(self.waited[eng].get(key, 0), 0)
        self._commit(ticket, reads, writes)
        self.n_inst += len(fns)
        return ticket

    def dma(self, eng, out, in_, reads, writes, semkey):
        waits = self._filter(eng, self._deps(reads, writes))
        key = ('dma', semkey)
        self.cnt[key] = self.cnt.get(key, 0) + 16
        ticket = (key, self.cnt[key])
        self.engs[eng].append((waits, [lambda e, o=out, i=in_: e.dma_start(out=o, in_=i)], key, 16))
        self._commit(ticket, reads, writes)
        self.n_inst += 1
        return ticket

    def final_wait(self, eng):
        waits = self._filter(eng, dict(self.cnt))
        self.engs[eng].append((waits, None, None, 0))


def build_program(Ss, Sp):
    Pq = Sp // 4
    segs = [dict(n='s', Skv=Ss, nq=Ss, kv_in_q=True), dict(n='p', Skv=Sp, nq=Pq, kv_in_q=False)]
    SKV_MAX = max(Ss, Sp)
    nc = bass.Bass("TRN2", target_bir_lowering=False)
    P = Prog()

    def din(name, shape, dt=F32):
        return nc.dram_tensor(name, list(shape), dt, kind="ExternalInput").ap()

    dr = {}
    for sg in segs:
        n = sg['n']
        dr[n + '_xq'] = din(n + '_xq', [(sg['nq'] + 2 * T) // T, 128, 8 * T])
        if not sg['kv_in_q']:
            dr[n + '_xkv'] = din(n + '_xkv', [sg['Skv'] // T, 128, 8 * T])
        dr[n + '_xtok'] = din(n + '_xtok', [sg['nq'], D])
        dr[n + '_ropekv'] = din(n + '_ropekv', [sg['Skv'], 64])
        dr[n + '_ropeq'] = din(n + '_ropeq', [sg['nq'], 64])
        dr[n + '_rv'] = din(n + '_rv', [128, 70])
        dr[n + '_memT'] = din(n + '_memT', [D, 256])
        dr[n + '_y'] = nc.dram_tensor(n + '_y', [sg['nq'], D], F32, kind="ExternalOutput").ap()
    wt = din('wt', [NT, 128, 4096])
    biastab = din('biastab', [128, 7 * 8 * 128])
    cmask_d = din('cmask', [128, 7 * 128])
    qn_d = din('qn', [128, 64])
    kn_d = din('kn', [128, 64])
    lng_d = din('lng', [128, D])
    lnb_d = din('lnb', [128, D])
    wscr = nc.dram_tensor('wscr', [NT, 128, 4096], BF16, kind="Internal").ap()

    es = ExitStack()
    with es:
        sb_total = [0]

        def sb(name, shape, dt):
            nb = int(np.prod(shape[1:])) * (2 if dt == BF16 else 4)
            sb_total[0] += nb
            return es.enter_context(nc.sbuf_tensor(name, list(shape), dt))

        KA = sb('KA', [128, SKV_MAX], BF16)
        VA = sb('VA', [128, SKV_MAX // 128, 2, 65], BF16)
        XT = sb('XT', [128, 3, 8, T], BF16)
        WR = sb('WR', [128, NSLOT, 4096], BF16)
        QA = sb('QA', [128, 2, 4, 128], BF16)
        SZA2 = [sb('SZA%d' % k, [64, 8, T], BF16) for k in range(2)]
        QB = sb('QB', [128, 4, T], BF16)
        SZB2 = [sb('SZB%d' % k, [64, 8, T], BF16) for k in range(2)]
        QM = sb('QM', [128, 4, T], BF16)
        SZM2 = [sb('SZM%d' % k, [128, 4, T], BF16) for k in range(2)]
        KB = sb('KB', [128, 4, 3, T], BF16)
        VB = sb('VB', [128, 3, 2, 8, 65], BF16)
        KM = sb('KM', [128, 4, 256], BF16)
        VM = sb('VM', [128, 2, 512], BF16)
        MG = sb('MG', [128, 8, T], BF16)
        PB = sb('PB', [128, NPB, 1024], BF16)
        ET = sb('ET', [128, 7, 8, 128], BF16)
        RV = sb('RV', [128, 2, 70], F32)
        XTOK = sb('XTOK', [128, D], F32)
        YB = sb('YB', [128, 2, D], F32)
        OSB = sb('OSB', [128, 1024], F32)
        RZ = sb('RZ', [128, 1024], F32)
        LNG = sb('LNG', [128, D], F32)
        LNB = sb('LNB', [128, D], F32)
        QN = sb('QN', [128, 64], F32)
        KN = sb('KN', [128, 64], F32)
        RT = sb('RT', [128, 2, 2, 64], F32)
        GT = sb('GT', [128, NPAR, 4, 32], F32)
        XS = sb('XS', [128, 512], F32)
        TMP = sb('TMP', [128, 2, 256], F32)
        ORO = sb('ORO', [128, 512], F32)
        QR = sb('QR', [128, 512], BF16)
        SS = sb('SS', [128, 24], F32)
        ST = sb('ST', [128, 2, 8], F32)
        SGT = sb('SGT', [128, 2, 512], BF16)
        MT = sb('MT', [128, 512], F32)
        IDENTF = sb('IDENTF', [128, 128], F32)
        IDENT = sb('IDENT', [128, 128], BF16)
        ONESF = sb('ONESF', [128, 64], F32)
        ONESB = sb('ONESB', [128, 128], BF16)
        EPS = sb('EPS', [128, 2], F32)
        PS = es.enter_context(nc.psum_tensor('PS', [128, 4096], F32))
        assert sb_total[0] <= 210000, sb_total[0]
        WKVA = MG

        P.const |= {'IDENT', 'ONESF', 'ONESB', 'EPS', 'VA1', 'VB1', 'QN', 'KN', 'LNG', 'LNB', 'ET'}
        xbank_ctr = [0]

        def next_bank():
            b = [6, 7, 4, 5, 0, 1, 2, 3][xbank_ctr[0] % 8]
            xbank_ctr[0] += 1
            return b

        P.emit('pool', lambda e: e.memset(VA[:, :, :, 64:65], 1.0), writes=['VA1'])
        P.emit('pool', lambda e: e.memset(VB[:, :, :, :, 64:65], 1.0), writes=['VB1'])
        P.emit('pool', lambda e: e.memset(ONESF[:], 1.0), writes=['ONESF'])
        P.emit('pool', lambda e: e.memset(ONESB[:], 1.0), writes=['ONESB'])
        P.emit('pool', lambda e: e.memset(EPS[:, 0:1], EPS_QK), writes=['EPS0'])
        P.emit('pool', lambda e: e.memset(EPS[:, 1:2], EPS_LN), reads=['EPS0'], writes=['EPS'])
        P.emit('pool', lambda e: e.memset(IDENTF[:], 0.0), writes=['IDENTF'])
        P.emit('pool', lambda e: e.affine_select(out=IDENTF[:], in_=IDENTF[:], pattern=[[-1, 128]],
                                                 compare_op=ALU.not_equal, fill=1.0, base=0,
                                                 channel_multiplier=1), reads=['IDENTF'], writes=['IDENTF'])
        P.emit('pool', lambda e: e.tensor_copy(out=IDENT[:], in_=IDENTF[:]), reads=['IDENTF'], writes=['IDENT'])
        P.dma('sp', QN[:], qn_d[:], [], ['QN'], 'c0')
        P.dma('sp', KN[:], kn_d[:], [], ['KN'], 'c1')
        P.dma('sp', LNG[:], lng_d[:], [], ['LNG'], 'c2')
        P.dma('sp', LNB[:], lnb_d[:], [], ['LNB'], 'c3')
        for si, sg in enumerate(segs):
            P.dma('sp', RV[:, si, :], dr[sg['n'] + '_rv'][:], [], [('RV', si)], ('c4', si))
            P.const.add(('RV', si))
        P.dma('sp', RZ[:, 0:896], cmask_d[:], [], ['RZ'], 'c6')
        for oi in range(7):
            P.dma('sp', OSB[:], biastab[:, oi * 1024:(oi + 1) * 1024], [], ['OSB'], 'c7')
            P.emit('act', lambda e: e.activation(out=OSB[:], in_=OSB[:], func=AF.Exp), reads=['OSB'], writes=['OSB'])
            P.emit('dve', lambda e: e.tensor_tensor(
                out=ET[:, oi, :, :], in0=OSB[:].rearrange("p (h q) -> p h q", h=8),
                in1=RZ[:, oi * 128:(oi + 1) * 128].unsqueeze(1).broadcast_to([128, 8, 128]), op=ALU.mult),
                reads=['OSB', 'RZ'], writes=[('ETw', oi)])
        P.emit('dve', lambda e: e.tensor_copy(out=SS[:, 0:1], in_=EPS[:, 0:1]),
               reads=[('ETw', oi) for oi in range(7)] + ['EPS'], writes=['ET'] + [('SS', q) for q in range(NPAR)])

        wr_g = [0]

        def wload(ti):
            slot = wr_g[0] % NSLOT
            wr_g[0] += 1
            nel = TILE_EL[ti]
            P.dma('sp', WR[:, slot, 0:nel], wscr[ti, :, 0:nel], [('wscr', ti)], [('WR', slot)], ('wr', slot))
            return slot

        conv_next = [0]
        conv_order = [27, 28] + list(range(0, 20)) + [24, 25]

        def convert_one():
            if conv_next[0] >= len(conv_order):
                return
            ti = conv_order[conv_next[0]]
            conv_next[0] += 1
            slot = wr_g[0] % NSLOT
            wr_g[0] += 1
            nel = TILE_EL[ti]
            P.dma('pool', WR[:, slot, 0:nel], wt[ti, :, 0:nel], [], [('WR', slot)], ('wrc', slot))
            P.dma('sp', wscr[ti, :, 0:nel], WR[:, slot, 0:nel], [('WR', slot)], [('wscr', ti)], ('ws', slot))

        xt_g = [0]

        def xt_load(src, col0):
            slot = xt_g[0] % 3
            xt_g[0] += 1
            P.dma('pool', XT[:, slot].rearrange("p c t -> p (c t)"), src[col0 // T], [], [('XT', slot)], ('xt', slot))
            return slot

        def rope_chain(H, gains_key, gains, rt_key, rt_ap, dst_keys, dst_ap, par=0, full=True):
            W = H * 32
            xo = 0 if full else par * 128
            to = 0 if full else par * 64
            sb0 = 0 if full else par * 2

            def K(name, *extra):
                if full:
                    return [(name,) + extra + (q,) for q in range(NPAR)]
                return [(name,) + extra + (par,)]
            xs_ap = XS[:, xo:xo + H * 64]
            cos = rt_ap[:, 0:32].rearrange("p (a f) -> p a f", a=2)
            sin = rt_ap[:, 32:64].rearrange("p (a f) -> p a f", a=2)
            g4 = gains[:].rearrange("p (a j f) -> p a j f", a=2, j=2)
            g1 = g4[:, :, 0, :]
            g2 = g4[:, :, 1, :]
            gp = 0 if full else par
            gt = [GT[:, gp, k, :].rearrange("p (a f) -> p a f", a=2) for k in range(4)]
            gk = [[('GT', k, gp)] for k in range(4)]
            P.emit('pool', lambda e: e.tensor_tensor(out=gt[0], in0=cos, in1=g1, op=ALU.mult), reads=[rt_key, gains_key], writes=gk[0])
            P.emit('pool', lambda e: e.tensor_tensor(out=gt[1], in0=sin, in1=g2, op=ALU.mult), reads=[rt_key, gains_key], writes=gk[1])
            P.emit('pool', lambda e: e.tensor_tensor(out=gt[2], in0=cos, in1=g2, op=ALU.mult), reads=[rt_key, gains_key], writes=gk[2])
            P.emit('pool', lambda e: e.tensor_tensor(out=gt[3], in0=sin, in1=g1, op=ALU.mult), reads=[rt_key, gains_key], writes=gk[3])
            x5 = xs_ap.rearrange("p (h a j f) -> p h a j f", h=H, a=2, j=2)
            x1 = x5[:, :, :, 0, :]
            x2 = x5[:, :, :, 1, :]
            oro = ORO[:, xo:xo + H * 64]
            o5 = oro.rearrange("p (h a j f) -> p h a j f", h=H, a=2, j=2)
            tm = [TMP[:, k, to:to + W].rearrange("p (h a f) -> p h a f", h=H, a=2) for k in range(2)]
            bcg = [g.unsqueeze(1).broadcast_to([128, H, 2, 16]) for g in gt]
            ssv = SS[:, sb0:sb0 + H]
            sdv = SS[:, 8 + sb0:8 + sb0 + H]
            rsv = SS[:, 16 + sb0:16 + sb0 + H]
            P.emit('dve', lambda e: e.tensor_tensor(out=oro, in0=xs_ap, in1=xs_ap, op=ALU.mult), reads=K('XS'), writes=K('ORO'))
            P.emit('dve', lambda e: e.tensor_reduce(out=ssv, in_=oro.rearrange("p (h d) -> p h d", h=H), axis=AX.X, op=ALU.add),
                   reads=K('ORO'), writes=K('SS'))
            P.emit('act', lambda e: e.activation(out=sdv, in_=ssv, func=AF.Sqrt, bias=EPS[:, 0:1], scale=1.0 / 64),
                   reads=K('SS') + ['EPS'], writes=K('SS'))
            P.emit('dve', lambda e: e.tensor_tensor(out=tm[0], in0=x1, in1=bcg[0], op=ALU.mult), reads=K('XS') + gk[0], writes=K('TMP', 0))
            P.emit('dve', lambda e: e.tensor_tensor(out=tm[1], in0=x2, in1=bcg[1], op=ALU.mult), reads=K('XS') + gk[1], writes=K('TMP', 1))
            P.emit('dve', lambda e: e.tensor_tensor(out=o5[:, :, :, 0, :], in0=tm[0], in1=tm[1], op=ALU.subtract),
                   reads=K('TMP', 0) + K('TMP', 1), writes=K('ORO'))
            P.emit('dve', lambda e: e.tensor_tensor(out=tm[0], in0=x2, in1=bcg[2], op=ALU.mult), reads=K('XS') + gk[2], writes=K('TMP', 0))
            P.emit('dve', lambda e: e.tensor_tensor(out=tm[1], in0=x1, in1=bcg[3], op=ALU.mult), reads=K('XS') + gk[3], writes=K('TMP', 1))
            P.emit('dve', lambda e: e.tensor_tensor(out=o5[:, :, :, 1, :], in0=tm[0], in1=tm[1], op=ALU.add),
                   reads=K('TMP', 0) + K('TMP', 1), writes=K('ORO'))
            P.emit('dve', lambda e: e.reciprocal(out=rsv, in_=sdv), reads=K('SS'), writes=K('SS'))
            P.emit('dve', lambda e: e.tensor_tensor(
                out=dst_ap.rearrange("p (h d) -> p h d", h=H), in0=oro.rearrange("p (h d) -> p h d", h=H),
                in1=rsv.unsqueeze(2).broadcast_to([128, H, 64]), op=ALU.mult),
                reads=K('ORO') + K('SS'), writes=dst_keys)

        pending_ng = [None]

        def ng_flush():
            if pending_ng[0] is not None:
                f = pending_ng[0]
                pending_ng[0] = None
                f()

        def norm_gate(sz_view, sz_keys):
            ng_flush()
            P.emit('dve', lambda e: e.tensor_copy(out=OSB[0:65, :], in_=PS[0:65, 2048:3072]),
                   reads=[('ps', 4), ('ps', 5)], writes=['OSB'])
            P.emit('dve', lambda e: e.reciprocal(out=OSB[64:65, :], in_=OSB[64:65, :]), reads=['OSB'], writes=['OSB'])

            def part1():
                P.emit('pe', [lambda e: e.matmul(PS[0:64, 3072:3584], lhsT=ONESF[64:65, 0:64], rhs=OSB[64:65, 0:512], start=True, stop=True),
                              lambda e: e.matmul(PS[0:64, 3584:4096], lhsT=ONESF[64:65, 0:64], rhs=OSB[64:65, 512:1024], start=True, stop=True)],
                       reads=['OSB', 'ONESF'], writes=[('ps', 6), ('ps', 7)])
                P.emit('dve', lambda e: e.tensor_tensor(out=RZ[0:64, :].rearrange("p (h q) -> p h q", h=8),
                                                        in0=PS[0:64, 3072:4096].rearrange("p (h q) -> p h q", h=8),
                                                        in1=sz_view, op=ALU.mult),
                       reads=[('ps', 6), ('ps', 7)] + sz_keys, writes=['RZ'])
                P.emit('dve', lambda e: e.tensor_tensor(out=sz_view, in0=OSB[0:64, :].rearrange("p (h q) -> p h q", h=8),
                                                        in1=RZ[0:64, :].rearrange("p (h q) -> p h q", h=8), op=ALU.mult),
                       reads=['OSB', 'RZ'], writes=sz_keys)
            pending_ng[0] = part1

        pb_g = [0]

        def next_pb():
            s = pb_g[0] % NPB
            pb_g[0] += 1
            return s

        jobs = collections.deque()
        mid_job = [False]

        def pop_job():
            f = jobs.popleft()
            f()
            mid_job[0] = getattr(f, 'half', 0) == 1
        try:
          for si, sg in enumerate(segs):
              n = sg['n']
              Skv, nq = sg['Skv'], sg['nq']
              nsteps = nq // T
              nl = nq // 128
              nkt = Skv // 128
              xq = dr[n + '_xq']
              if sg['kv_in_q']:
                  kvsrc, kvoff = xq, T
              else:
                  kvsrc, kvoff = dr[n + '_xkv'], 0
              ropekv = dr[n + '_ropekv'].rearrange("(c p) f -> p c f", p=128)
              ropeq = dr[n + '_ropeq'].rearrange("(c p) f -> p c f", p=128)
              xtok = dr[n + '_xtok']
              ydr = dr[n + '_y']

              _phase('phase1')
              P.dma('pool', WKVA[:].rearrange("p c n -> p (c n)"), wt[26, :, 0:2048], [], [('MG', c) for c in range(8)], 'c5')
              mgk = [('MG', c) for c in range(8)]
              for ci in range(Skv // T):
                  if si == 0:
                      convert_one()
                  xs_slot = xt_load(kvsrc, kvoff + ci * T)
                  rb = ci % 2
                  P.dma('sp', RT[:, rb], ropekv[:, ci * 2:ci * 2 + 2, :], [], [('RT', rb)], ('rt', rb))
                  for tt in range(2):
                      kt = ci * 2 + tt
                      b = next_bank()
                      P.emit('pe', [lambda e, c=c: e.matmul(
                          PS[:, b * 512:b * 512 + 256], lhsT=XT[:, xs_slot, c, tt * 128:(tt + 1) * 128], rhs=WKVA[:, c, :],
                          start=(c == 0), stop=(c == 7)) for c in range(8)],
                          reads=[('XT', xs_slot)] + mgk, writes=[('ps', b)])
                      kp = kt % NPAR
                      P.emit('act', lambda e: e.activation(out=XS[:, kp * 128:kp * 128 + 128], in_=PS[:, b * 512:b * 512 + 128], func=AF.Copy),
                             reads=[('ps', b)], writes=[('XS', kp)])
                      P.emit('act', lambda e: e.activation(
                          out=VA[:, kt, :, 0:64], in_=PS[:, b * 512 + 128:b * 512 + 256].rearrange("p (k d) -> p k d", k=2), func=AF.Copy),
                          reads=[('ps', b)], writes=[('VA', kt)])
                      rope_chain(2, 'KN', KN, ('RT', rb), RT[:, rb, tt, :], [('QR', kp)], QR[:, kp * 128:kp * 128 + 128], par=kp, full=False)
                      b2 = next_bank()
                      P.emit('pe', lambda e: e.transpose(
                          PS[:, b2 * 512:b2 * 512 + 64].bitcast(BF16), QR[:, kp * 128:kp * 128 + 128], IDENT[:]),
                          reads=[('QR', kp), 'IDENT'], writes=[('ps', b2)])
                      P.emit('act', lambda e: e.activation(
                          out=KA[:, kt * 128:(kt + 1) * 128], in_=PS[:, b2 * 512:b2 * 512 + 64].bitcast(BF16), func=AF.Copy),
                          reads=[('ps', b2)], writes=[('KA', kt)])
              if si == 0:
                  while conv_next[0] < len(conv_order):
                      convert_one()

              _phase('memkv')
              ms = xt_g[0] % 3
              xt_g[0] += 1
              P.dma('pool', XT[:, ms], dr[n + '_memT'].rearrange("(c p) t -> p c t", p=128), [], [('XT', ms)], ('xt', ms))
              ws = wload(27)
              for h in range(4):
                  b = next_bank()
                  P.emit('pe', [lambda e, c=c: e.matmul(
                      PS[:, b * 512:b * 512 + 256], lhsT=WR[:, ws, c * 512 + h * 128:c * 512 + (h + 1) * 128], rhs=XT[:, ms, c, :],
                      start=(c == 0), stop=(c == 7)) for c in range(8)],
                      reads=[('XT', ms), ('WR', ws)], writes=[('ps', b)])
                  P.emit('dve', lambda e: e.tensor_copy(out=KM[:, h, :], in_=PS[:, b * 512:b * 512 + 256]),
                         reads=[('ps', b)], writes=[('KM', h)])
              ws = wload(28)
              for mt in range(2):
                  b = next_bank()
                  P.emit('pe', [lambda e, c=c: e.matmul(
                      PS[:, b * 512:(b + 1) * 512], lhsT=XT[:, ms, c, mt * 128:(mt + 1) * 128], rhs=WR[:, ws, c * 512:(c + 1) * 512],
                      start=(c == 0), stop=(c == 7)) for c in range(8)],
                      reads=[('XT', ms), ('WR', ws)], writes=[('ps', b)])
                  P.emit('dve', lambda e: e.tensor_copy(out=VM[:, mt, :], in_=PS[:, b * 512:(b + 1) * 512]),
                         reads=[('ps', b)], writes=[('VM', mt)])

              _phase('kvb')
              def kvb(jj, xs_):
                  rs = (jj + 1) % 3
                  ws = wload(0)
                  for p in range(4):
                      b = next_bank()
                      P.emit('pe', [lambda e, c=c: e.matmul(
                          PS[:, b * 512:b * 512 + T], lhsT=WR[:, ws, c * 512 + p * 128:c * 512 + (p + 1) * 128], rhs=XT[:, xs_, c, :],
                          start=(c == 0), stop=(c == 7)) for c in range(8)],
                          reads=[('XT', xs_), ('WR', ws)], writes=[('ps', b)])
                      P.emit('dve', lambda e: e.tensor_copy(out=KB[:, p, rs, :], in_=PS[:, b * 512:b * 512 + T]),
                             reads=[('ps', b)], writes=[('KB', rs, p)])
                  ws = wload(1)
                  for tt in range(2):
                      b = next_bank()
                      P.emit('pe', [lambda e, c=c: e.matmul(
                          PS[:, b * 512:(b + 1) * 512], lhsT=XT[:, xs_, c, tt * 128:(tt + 1) * 128], rhs=WR[:, ws, c * 512:(c + 1) * 512],
                          start=(c == 0), stop=(c == 7)) for c in range(8)],
                          reads=[('XT', xs_), ('WR', ws)], writes=[('ps', b)])
                      P.emit('dve', lambda e: e.tensor_copy(
                          out=VB[:, rs, tt, :, 0:64], in_=PS[:, b * 512:(b + 1) * 512].rearrange("p (h d) -> p h d", h=8)),
                          reads=[('ps', b)], writes=[('VB', rs, tt)])

              def fm_proj(ti, ngroups, M, xs_, evac):
                  ws = wload(ti)
                  for g in range(ngroups):
                      b = next_bank()
                      P.emit('pe', [lambda e, c=c: e.matmul(
                          PS[0:M, b * 512:b * 512 + T], lhsT=WR[:, ws, c * 512 + g * M:c * 512 + (g + 1) * M], rhs=XT[:, xs_, c, :],
                          start=(c == 0), stop=(c == 7)) for c in range(8)],
                          reads=[('XT', xs_), ('WR', ws)], writes=[('ps', b)])
                      evac(g, b)

              def fm_proj_zpair(ti, xs_, dst, keyname):
                ws = wload(ti)
                for p in range(4):
                    b = next_bank()
                    P.emit('pe', [lambda e, c=c: e.matmul(
                        PS[:, b * 512:b * 512 + T], lhsT=WR[:, ws, c * 512 + p * 128:c * 512 + (p + 1) * 128], rhs=XT[:, xs_, c, :],
                        start=(c == 0), stop=(c == 7)) for c in range(8)],
                        reads=[('XT', xs_), ('WR', ws)], writes=[('ps', b)])
                    P.emit('act', lambda e: e.activation(out=dst[:, 2 * p, :], in_=PS[0:64, b * 512:b * 512 + T], func=AF.Silu),
                           reads=[('ps', b)], writes=[(keyname, 2 * p)])
                    P.emit('act', lambda e: e.activation(out=dst[:, 2 * p + 1, :], in_=PS[64:128, b * 512:b * 512 + T], func=AF.Silu),
                           reads=[('ps', b)], writes=[(keyname, 2 * p + 1)])

              def ev_silu(dst, keyname, M):
                  def f(g, b):
                      P.emit('act', lambda e: e.activation(out=dst[:, g, :], in_=PS[0:M, b * 512:b * 512 + T], func=AF.Silu),
                             reads=[('ps', b)], writes=[(keyname, g)])
                  return f

              def ev_copy(dst, keyname):
                  def f(g, b):
                      P.emit('dve', lambda e: e.tensor_copy(out=dst[:, g, :], in_=PS[:, b * 512:b * 512 + T]),
                             reads=[('ps', b)], writes=[(keyname, g)])
                  return f

              xslot = {}
              xslot[-1] = xt_load(xq, 0)
              xslot[0] = xt_load(xq, T)
              kvb(-1, xslot[-1])
              kvb(0, xslot[0])
              for j in range(nsteps):
                  xslot[j + 1] = xt_load(xq, (j + 2) * T)
                  xc = xslot[j]
                  kvb(j + 1, xslot[j + 1])
                  _phase('proj')
                  rb = j % 2
                  P.dma('sp', RT[:, rb], ropeq[:, j * 2:j * 2 + 2, :], [], [('RT', rb)], ('rt', rb))
                  ws = wload(2)
                  for u in range(2):
                      b = next_bank()
                      P.emit('pe', [lambda e, c=c: e.matmul(
                          PS[:, b * 512:(b + 1) * 512], lhsT=XT[:, xc, c, u * 128:(u + 1) * 128], rhs=WR[:, ws, c * 512:(c + 1) * 512],
                          start=(c == 0), stop=(c == 7)) for c in range(8)],
                          reads=[('XT', xc), ('WR', ws)], writes=[('ps', b)])
                      P.emit('act', lambda e: e.activation(out=XS[:], in_=PS[:, b * 512:(b + 1) * 512], func=AF.Copy),
                             reads=[('ps', b)], writes=[('XS', q) for q in range(NPAR)])
                      rope_chain(8, 'QN', QN, ('RT', rb), RT[:, rb, u, :], [('QR', q) for q in range(NPAR)], QR[:])
                      b2 = next_bank()
                      P.emit('pe', [lambda e, s=s: e.transpose(
                          PS[:, b2 * 512 + s * 64:b2 * 512 + (s + 1) * 64].bitcast(BF16), QR[:, s * 128:(s + 1) * 128], IDENT[:])
                          for s in range(4)], reads=[('QR', q) for q in range(NPAR)] + ['IDENT'], writes=[('ps', b2)])
                      P.emit('dve', lambda e: e.tensor_copy(
                          out=QA[:, u, :, :].rearrange("p s q -> p (s q)"), in_=PS[:, b2 * 512:b2 * 512 + 256].bitcast(BF16)),
                          reads=[('ps', b2)], writes=[('QA', u)])
                  sp_ = j % 2
                  SZA, SZB, SZM = SZA2[sp_], SZB2[sp_], SZM2[sp_]
                  ng_flush()
                  fm_proj_zpair(3, xc, SZA, ('SZA', sp_))
                  fm_proj(4, 4, 128, xc, ev_copy(QB, 'QB'))
                  fm_proj_zpair(5, xc, SZB, ('SZB', sp_))
                  fm_proj(6, 4, 128, xc, ev_copy(QM, 'QM'))
                  fm_proj(7, 4, 128, xc, ev_silu(SZM, ('SZM', sp_), 128))

                  sza_keys = [(('SZA', sp_), g) for g in range(8)]
                  szb_keys = [(('SZB', sp_), g) for g in range(8)]
                  szm_keys = [(('SZM', sp_), g) for g in range(4)]
                  _phase('attB')
                  for sub in range(2):
                      i = 2 * j + sub
                      cls = blk_class(i, nl)
                      offs = OFFS[cls]
                      def b_geom(oidx):
                          kb_ = i + offs[oidx]
                          return (kb_ // 2 + 1) % 3, kb_ % 2

                      def b_qk(oidx):
                          rs, ksub = b_geom(oidx)
                          sb_ = oidx % 2
                          P.emit('pe', [lambda e, h=h: e.matmul(
                              PS[:, sb_ * 1024 + (h % 2) * 512 + (h // 2) * 128:sb_ * 1024 + (h % 2) * 512 + (h // 2 + 1) * 128],
                              lhsT=KB[(h % 2) * 64:(h % 2) * 64 + 64, h // 2, rs, ksub * 128:(ksub + 1) * 128],
                              rhs=QB[(h % 2) * 64:(h % 2) * 64 + 64, h // 2, sub * 128:(sub + 1) * 128], start=True, stop=True)
                              for h in range(8)],
                              reads=[('KB', rs, p) for p in range(4)] + [('QB', p) for p in range(4)],
                              writes=[('ps', 2 * sb_), ('ps', 2 * sb_ + 1)])

                      def b_rest(oidx):
                          rs, ksub = b_geom(oidx)
                          sb_ = oidx % 2
                          o = offs[oidx]
                          p0 = next_pb()
                          p1 = next_pb()
                          P.emit('act', lambda e: e.activation(out=PB[:, p0, :], in_=PS[:, sb_ * 1024:(sb_ + 1) * 1024], func=AF.Exp, scale=0.125),
                                 reads=[('ps', 2 * sb_), ('ps', 2 * sb_ + 1)], writes=[('PB', p0)])
                          oi = o + 3
                          for a in range(2):
                              col = cls * 14 + oi * 2 + a
                              P.emit('dve', lambda e: e.scalar_tensor_tensor(
                                  out=PB[:, p1, :].rearrange("p (h q) -> p h q", h=8)[:, :, a * 64:(a + 1) * 64],
                                  in0=PB[:, p0, :].rearrange("p (h q) -> p h q", h=8)[:, :, a * 64:(a + 1) * 64],
                                  scalar=RV[:, si, col:col + 1], in1=ET[:, oi, :, a * 64:(a + 1) * 64],
                                  op0=ALU.mult, op1=ALU.mult),
                                  reads=[('PB', p0), ('RV', si), 'ET'], writes=[('PB', p1)])
                          P.emit('pe', [lambda e, h=h: e.matmul(
                              PS[0:65, 2048 + h * 128:2048 + (h + 1) * 128], lhsT=VB[:, rs, ksub, h, :],
                              rhs=PB[:, p1, ((h % 2) * 4 + h // 2) * 128:((h % 2) * 4 + h // 2 + 1) * 128],
                              start=(oidx == 0 and h % 4 == 0), stop=(oidx == len(offs) - 1),
                              skip_group_check=True)
                              for h in range(8)],
                              reads=[('VB', rs, ksub), 'VB1', ('PB', p1)], writes=[('ps', 4), ('ps', 5)])
                      b_qk(0)
                      b_qk(1)
                      for oidx in range(len(offs)):
                          if oidx == 2:
                              ng_flush()
                          b_rest(oidx)
                          if oidx + 2 < len(offs):
                              b_qk(oidx + 2)
                      norm_gate(SZB[:, :, sub * 128:(sub + 1) * 128], szb_keys)

                  _phase('attM')
                  ng_flush()
                  for h in range(4):
                      sb_ = h % 2
                      P.emit('pe', [lambda e, mt=mt: e.matmul(
                          PS[:, sb_ * 1024 + mt * T:sb_ * 1024 + (mt + 1) * T], lhsT=KM[:, h, mt * 128:(mt + 1) * 128], rhs=QM[:, h, :],
                          start=True, stop=True) for mt in range(2)],
                          reads=[('KM', h), ('QM', h)], writes=[('ps', 2 * sb_)])
                      pbs = next_pb()
                      P.emit('act', lambda e: e.activation(out=PB[:, pbs, 0:512], in_=PS[:, sb_ * 1024:sb_ * 1024 + 512], func=AF.Exp,
                                                           scale=float(128 ** -0.5)),
                             reads=[('ps', 2 * sb_)], writes=[('PB', pbs)])
                      ob = 4 + h // 2
                      xb = 6 + h // 2
                      oc = (h % 2) * T
                      P.emit('pe', [lambda e, mt=mt: e.matmul(
                          PS[:, ob * 512 + oc:ob * 512 + oc + T], lhsT=VM[:, mt, h * 128:(h + 1) * 128], rhs=PB[:, pbs, mt * T:(mt + 1) * T],
                          start=(mt == 0), stop=(mt == 1)) for mt in range(2)] + [lambda e, mt=mt: e.matmul(
                              PS[:, xb * 512 + oc:xb * 512 + oc + T], lhsT=ONESB[:], rhs=PB[:, pbs, mt * T:(mt + 1) * T],
                              start=(mt == 0), stop=(mt == 1)) for mt in range(2)],
                          reads=[('VM', 0), ('VM', 1), ('PB', pbs), 'ONESB'], writes=[('ps', ob), ('ps', xb)])
                  P.emit('dve', lambda e: e.reciprocal(out=OSB[:], in_=PS[:, 3072:4096]),
                         reads=[('ps', 6), ('ps', 7)], writes=['OSB'])
                  P.emit('dve', lambda e: e.tensor_tensor(out=RZ[:], in0=OSB[:], in1=SZM[:].rearrange("p h t -> p (h t)"), op=ALU.mult),
                         reads=['OSB'] + szm_keys, writes=['RZ'])
                  P.emit('dve', lambda e: e.tensor_tensor(out=SZM[:].rearrange("p h t -> p (h t)"), in0=PS[:, 2048:3072], in1=RZ[:], op=ALU.mult),
                         reads=['RZ', ('ps', 4), ('ps', 5)], writes=szm_keys)

                  _phase('attA')
                  for u in range(2):
                      def qk(kt):
                          sb_ = kt % 2
                          P.emit('pe', [
                              lambda e: e.matmul(PS[:, sb_ * 1024:sb_ * 1024 + 512], lhsT=KA[0:64, kt * 128:(kt + 1) * 128],
                                                 rhs=QA[0:64, u, :, :].rearrange("p s q -> p (s q)"), start=True, stop=True),
                              lambda e: e.matmul(PS[:, sb_ * 1024 + 512:sb_ * 1024 + 1024], lhsT=KA[64:128, kt * 128:(kt + 1) * 128],
                                                 rhs=QA[64:128, u, :, :].rearrange("p s q -> p (s q)"), start=True, stop=True)],
                              reads=[('KA', kt), ('QA', u)], writes=[('ps', 2 * sb_), ('ps', 2 * sb_ + 1)])

                      def ex_pv(kt):
                          sb_ = kt % 2
                          pbs = next_pb()
                          P.emit('act', lambda e: e.activation(out=PB[:, pbs, :], in_=PS[:, sb_ * 1024:(sb_ + 1) * 1024], func=AF.Exp, scale=0.125),
                                 reads=[('ps', 2 * sb_), ('ps', 2 * sb_ + 1)], writes=[('PB', pbs)])
                          P.emit('pe', [
                              lambda e: e.matmul(PS[0:65, 2048:2560], lhsT=VA[:, kt, 0, :], rhs=PB[:, pbs, 0:512],
                                                 start=(kt == 0), stop=(kt == nkt - 1)),
                              lambda e: e.matmul(PS[0:65, 2560:3072], lhsT=VA[:, kt, 1, :], rhs=PB[:, pbs, 512:1024],
                                                 start=(kt == 0), stop=(kt == nkt - 1))],
                              reads=[('VA', kt), 'VA1', ('PB', pbs)], writes=[('ps', 4), ('ps', 5)])
                      qk(0)
                      qk(1)
                      for kt in range(nkt):
                          if kt == min(8, nkt - 1):
                              if mid_job[0] and jobs:
                                  pop_job()
                              ng_flush()
                          if kt % 2 == 1 and jobs:
                              pop_job()
                          ex_pv(kt)
                          if kt + 2 < nkt:
                              qk(kt + 2)
                      norm_gate(SZA[:, :, u * 128:(u + 1) * 128], sza_keys)

                  _phase('merge')
                  while jobs:
                      pop_job()

                  def build_jobs(j=j, xc=xc, SZA=SZA, SZB=SZB, SZM=SZM, sza_keys=sza_keys, szb_keys=szb_keys,
                                 szm_keys=szm_keys, n=n, ydr=ydr, xtok=xtok):
                      out = []
                      jk = [0]
                      order = [(tt, ct, nb_) for tt in range(2) for ct in range(2) for nb_ in (2, 1, 0)]
                      NO = len(order)
                      allslots = [dict() for _ in range(NO + 1)]
                      MBv = [XS[:].bitcast(BF16), ORO[:].bitcast(BF16)]
                      mbk = [[('XS', q) for q in range(NPAR)], [('ORO', q) for q in range(NPAR)]]

                      def ensure(g_, which):
                          sl = allslots[g_]
                          if which not in sl:
                              if g_ < NO:
                                  _, ct, nb_ = order[g_]
                                  sl[which] = wload(8 + ct * 6 + nb_ * 2 + (0 if which == 'wg' else 1))
                              else:
                                  sl[which] = wload(24 if which == 'wg' else 25)
                      for g_, (tt, ct, nb_) in enumerate(order):
                          slots = allslots[g_]
                          idx = g_ % 3
                          if True:
                              def p1(g_=g_, ct=ct, nb_=nb_, slots=slots, tt=tt):
                                  ensure(g_, 'wg')
                                  ensure(g_, 'wb')
                                  wg = slots['wg']
                                  P.emit('pe', [lambda e, c=c: e.matmul(
                                      PS[:, 3072:3584], lhsT=XT[:, xc, c, tt * 128:(tt + 1) * 128], rhs=WR[:, wg, c * 512:(c + 1) * 512],
                                      start=(c == 0), stop=(c == 7)) for c in range(8)],
                                      reads=[('XT', xc), ('WR', wg)], writes=[('ps', 6)])
                                  ensure(g_ + 1, 'wg')

                              def p2(g_=g_, ct=ct, nb_=nb_, slots=slots, tt=tt, idx=idx):
                                  wb = slots['wb']
                                  k = jk[0] % 2
                                  jk[0] += 1
                                  if nb_ == 0:
                                      P.emit('pe', [lambda e, h=h: e.matmul(
                                          PS[:, 3584:4096], lhsT=SZA[:, h, tt * 128:(tt + 1) * 128], rhs=WR[0:64, wb, h * 512:(h + 1) * 512],
                                          start=(h == 0), stop=(h == 7)) for h in range(8)],
                                          reads=[('WR', wb)] + sza_keys, writes=[('ps', 7)])
                                  elif nb_ == 1:
                                      P.emit('pe', [lambda e, h=h: e.matmul(
                                          PS[:, 3584:4096], lhsT=SZB[:, h, tt * 128:(tt + 1) * 128], rhs=WR[0:64, wb, h * 512:(h + 1) * 512],
                                          start=(h == 0), stop=(h == 7)) for h in range(8)],
                                          reads=[('WR', wb)] + szb_keys, writes=[('ps', 7)])
                                  else:
                                      P.emit('pe', [lambda e, h=h: e.matmul(
                                          PS[:, 3584:4096], lhsT=SZM[:, h, tt * 128:(tt + 1) * 128], rhs=WR[:, wb, h * 512:(h + 1) * 512],
                                          start=(h == 0), stop=(h == 3)) for h in range(4)],
                                          reads=[('WR', wb)] + szm_keys, writes=[('ps', 7)])
                                  ensure(g_ + 1, 'wb')
                                  P.emit('act', lambda e: e.activation(out=SGT[:, k, :], in_=PS[:, 3072:3584], func=AF.Tanh, scale=0.5),
                                         reads=[('ps', 6)], writes=[('SGT', k)])
                                  acc = YB[:, tt, ct * 512:(ct + 1) * 512]
                                  yk = ('YB', tt)
                                  if idx == 0:
                                      P.emit('dve', lambda e: e.scalar_tensor_tensor(
                                          out=acc, in0=SGT[:, k, :], scalar=1.0, in1=PS[:, 3584:4096], op0=ALU.add, op1=ALU.mult),
                                          reads=[('SGT', k), ('ps', 7)], writes=[yk])
                                  else:
                                      P.emit('dve', lambda e: e.scalar_tensor_tensor(
                                          out=MT[:], in0=SGT[:, k, :], scalar=1.0, in1=PS[:, 3584:4096], op0=ALU.add, op1=ALU.mult),
                                          reads=[('SGT', k), ('ps', 7)], writes=['MT'])
                                      if idx == 1:
                                          P.emit('dve', lambda e: e.tensor_tensor(out=acc, in0=acc, in1=MT[:], op=ALU.add),
                                                 reads=[yk, 'MT'], writes=[yk])
                                      else:
                                          P.emit('dve', lambda e: e.tensor_tensor(out=MT[:], in0=acc, in1=MT[:], op=ALU.add),
                                                 reads=[yk, 'MT'], writes=['MT'])
                                          P.emit('dve', lambda e: e.tensor_scalar(out=MBv[tt][:, ct * 512:(ct + 1) * 512], in0=MT[:], scalar1=0.5,
                                                                                  scalar2=None, op0=ALU.mult),
                                                 reads=['MT'], writes=mbk[tt])
                              p1.half = 1
                              p2.half = 2
                              out.append(p1)
                              out.append(p2)
                      for tt in range(2):
                          def tr(tt=tt):
                              P.emit('pe', [lambda e, c=c: e.transpose(
                                  PS[:, 3072 + c * 64:3072 + (c + 1) * 64].bitcast(BF16), MBv[tt][:, c * 128:(c + 1) * 128], IDENT[:])
                                  for c in range(8)], reads=mbk[tt] + ['IDENT'], writes=[('ps', 6)])
                              P.emit('dve', lambda e: e.tensor_copy(
                                  out=MG[:, :, tt * 128:(tt + 1) * 128],
                                  in_=PS[:, 3072:3584].bitcast(BF16).rearrange("p (c q) -> p c q", c=8)),
                                  reads=[('ps', 6)], writes=[('MG', c) for c in range(8)])

                          def o1(tt=tt):
                              wo0 = allslots[NO]['wg']
                              P.emit('pe', [lambda e, c=c: e.matmul(
                                  PS[:, 3072:3584], lhsT=MG[:, c, tt * 128:(tt + 1) * 128], rhs=WR[:, wo0, c * 512:(c + 1) * 512],
                                  start=(c == 0), stop=(c == 7)) for c in range(8)],
                                  reads=[('MG', c) for c in range(8)] + [('WR', wo0)], writes=[('ps', 6)])

                          def o2(tt=tt):
                              wo1 = allslots[NO]['wb']
                              tok0 = j * T + tt * 128
                              P.emit('pe', [lambda e, c=c: e.matmul(
                                  PS[:, 3584:4096], lhsT=MG[:, c, tt * 128:(tt + 1) * 128], rhs=WR[:, wo1, c * 512:(c + 1) * 512],
                                  start=(c == 0), stop=(c == 7)) for c in range(8)],
                                  reads=[('MG', c) for c in range(8)] + [('WR', wo1)], writes=[('ps', 7)])
                              P.dma('sp', XTOK[:], xtok[tok0:tok0 + 128, :], [], ['XTOK'], 'xtok')
                              for ct in range(2):
                                  P.emit('dve', lambda e: e.scalar_tensor_tensor(
                                      out=YB[:, tt, ct * 512:(ct + 1) * 512], in0=XTOK[:, ct * 512:(ct + 1) * 512], scalar=float(DN_ALPHA),
                                      in1=PS[:, 3072 + ct * 512:3072 + (ct + 1) * 512], op0=ALU.mult, op1=ALU.add),
                                      reads=['XTOK', ('ps', 6 + ct)], writes=[('YB', tt)])
                          tr.half = 0
                          o1.half = 1
                          o2.half = 2
                          out.append(tr)
                          out.append(o1)
                          out.append(o2)

                      def ln_tail(part):
                          for tt in range(2):
                              tok0 = j * T + tt * 128
                              yk = ('YB', tt)
                              sk = ('ST', tt)
                              if part == 0:
                                  P.emit('dve', lambda e: e.tensor_reduce(out=ST[:, tt, 0:1], in_=YB[:, tt, :], axis=AX.X, op=ALU.add),
                                         reads=[yk], writes=[sk])
                                  P.emit('dve', lambda e: e.tensor_scalar(out=ST[:, tt, 1:2], in0=ST[:, tt, 0:1], scalar1=-1.0 / D, scalar2=None, op0=ALU.mult),
                                         reads=[sk], writes=[sk])
                                  P.emit('dve', lambda e: e.tensor_scalar(out=YB[:, tt, :], in0=YB[:, tt, :], scalar1=ST[:, tt, 1:2], scalar2=None, op0=ALU.add),
                                         reads=[sk, yk], writes=[yk])
                                  P.emit('pool', lambda e: e.tensor_tensor(out=XTOK[:], in0=YB[:, tt, :], in1=YB[:, tt, :], op=ALU.mult),
                                         reads=[yk], writes=['XTOK'])
                                  P.emit('dve', lambda e: e.tensor_reduce(out=ST[:, tt, 2:3], in_=XTOK[:], axis=AX.X, op=ALU.add),
                                         reads=['XTOK'], writes=[sk])
                              else:
                                  P.emit('act', lambda e: e.activation(out=ST[:, tt, 3:4], in_=ST[:, tt, 2:3], func=AF.Sqrt, bias=EPS[:, 1:2], scale=1.0 / D),
                                         reads=[sk, 'EPS'], writes=[sk])
                                  P.emit('dve', lambda e: e.reciprocal(out=ST[:, tt, 4:5], in_=ST[:, tt, 3:4]), reads=[sk], writes=[sk])
                                  P.emit('dve', lambda e: e.scalar_tensor_tensor(out=YB[:, tt, :], in0=YB[:, tt, :], scalar=ST[:, tt, 4:5], in1=LNG[:],
                                                                                 op0=ALU.mult, op1=ALU.mult),
                                         reads=[sk, yk, 'LNG'], writes=[yk])
                                  P.emit('pool', lambda e: e.tensor_tensor(out=YB[:, tt, :], in0=YB[:, tt, :], in1=LNB[:], op=ALU.add),
                                         reads=[yk, 'LNB'], writes=[yk])
                                  P.dma('pool', ydr[tok0:tok0 + 128, :], YB[:, tt, :], [yk], [('ydr', n, j, tt)], ('yst', tt))
                      out.append(lambda: ln_tail(0))
                      out.append(lambda: None)
                      out.append(lambda: ln_tail(1))
                      return out
                  jobs.extend(build_jobs())
              ng_flush()
              while jobs:
                  pop_job()

        except _StopEmission:
            pass
        P.final_wait('pool')

        sems = {}
        for k in P.cnt:
            sems[k] = es.enter_context(nc.semaphore("s%d" % len(sems)))
        block = es.enter_context(nc.Block())

        def run(name, e):
            for waits, fns, key, inc in P.engs[name]:
                for (k, v) in waits:
                    e.wait_ge(sems[k], v)
                if fns is None:
                    continue
                for f in fns[:-1]:
                    f(e)
                fns[-1](e).then_inc(sems[key], inc)

        @block.tensor
        def _(e):
            run('pe', e)

        @block.scalar
        def _(e):
            run('act', e)

        @block.vector
        def _(e):
            run('dve', e)

        @block.gpsimd
        def _(e):
            run('pool', e)

        @block.sync
        def _(e):
            run('sp', e)
    return nc, P


def _tile_k8(W):
    C = W.shape[1]
    return np.ascontiguousarray(W.reshape(8, 128, C).transpose(1, 0, 2).reshape(128, 8 * C))


def _weight_tiles(w_in, w_mem_kv, w_branch, w_out):
    wt = np.zeros((NT, 128, 4096), np.float32)
    qa_cols = np.concatenate([np.arange(h * 64, (h + 1) * 64) for h in (0, 4, 1, 5, 2, 6, 3, 7)])
    wt[0] = _tile_k8(w_in[:, 1792:2304])
    wt[1] = _tile_k8(w_in[:, 2304:2816])
    wt[2] = _tile_k8(w_in[:, 0:512][:, qa_cols])
    wt[3] = _tile_k8(w_in[:, 768:1280])
    wt[4] = _tile_k8(w_in[:, 1280:1792])
    wt[5] = _tile_k8(w_in[:, 2816:3328])
    wt[6] = _tile_k8(w_in[:, 3328:3840])
    wt[7] = _tile_k8(w_in[:, 3840:4352])
    for ct in range(2):
        for nb in range(3):
            base = 8 + ct * 6 + nb * 2
            wt[base] = _tile_k8(w_in[:, 4352 + nb * 1024 + ct * 512:4352 + nb * 1024 + (ct + 1) * 512])
            wb = w_branch[nb][:, ct * 512:(ct + 1) * 512]
            if nb < 2:
                wt[base + 1, 0:64, :] = wb.reshape(8, 64, 512).transpose(1, 0, 2).reshape(64, 4096)
            else:
                wt[base + 1, :, 0:2048] = wb.reshape(4, 128, 512).transpose(1, 0, 2).reshape(128, 2048)
    wt[24] = _tile_k8(w_out[:, 0:512])
    wt[25] = _tile_k8(w_out[:, 512:1024])
    wt[26, :, 0:2048] = _tile_k8(w_in[:, 512:768])
    wt[27] = _tile_k8(w_mem_kv[:, 0:512])
    wt[28] = _tile_k8(w_mem_kv[:, 512:1024])
    return wt


def _rope_table(pos):
    pos2 = np.stack([pos // GRID_W, pos % GRID_W], axis=-1).astype(np.float32)
    inv_freq = (np.float32(10000.0) ** (-np.arange(16, dtype=np.float32) / np.float32(16))).astype(np.float32)
    ang = (pos2[:, :, None] * inv_freq).astype(np.float32)
    return np.ascontiguousarray(np.concatenate([np.cos(ang).reshape(-1, 32), np.sin(ang).reshape(-1, 32)], axis=1).astype(np.float32))


def _rv_table(S, q0, nq):
    R = S // GRID_W
    nl = nq // 128
    rv = np.zeros((128, 5, 7, 2), np.float32)
    kr_l = np.arange(128) // 64
    for i in range(nl):
        cls = blk_class(i, nl)
        gi = q0 // 128 + i
        for o in OFFS[cls]:
            for a in range(2):
                qr = 2 * gi + a
                q_wr = min(max(qr - 4, 0), R - 8)
                kr = 2 * (gi + o) + kr_l
                ok = (kr >= 0) & (kr < R) & (kr >= q_wr) & (kr < q_wr + 8)
                rv[:, cls, o + 3, a] = ok.astype(np.float32)
    return np.ascontiguousarray(rv.reshape(128, 70))


def _bias_tables(rpb):
    k = np.arange(128)
    q = np.arange(128)
    krl, kc = k // 64, k % 64
    qrl, qc = q // 64, q % 64
    bt = np.zeros((128, 7, 8, 128), np.float32)
    cm = np.zeros((128, 7, 128), np.float32)
    q_wc = np.clip(qc - 8, 0, GRID_W - 16)
    dc = kc[:, None] - qc[None, :] + 15
    col_ok = (kc[:, None] >= q_wc[None, :]) & (kc[:, None] < q_wc[None, :] + 16)
    for oi in range(7):
        o = oi - 3
        drr = 2 * o + krl[:, None] - qrl[None, :] + 7
        ok = col_ok & (drr >= 0) & (drr <= 14) & (dc >= 0) & (dc <= 30)
        cm[:, oi, :] = ok
        bt[:, oi, :, :] = rpb[[0, 2, 4, 6, 1, 3, 5, 7]][:, np.clip(drr, 0, 14), np.clip(dc, 0, 30)].transpose(1, 0, 2)
    return np.ascontiguousarray(bt.reshape(128, 7 * 8 * 128)), np.ascontiguousarray(cm.reshape(128, 7 * 128))


def _xt_tiles(xT):
    ns = xT.shape[1] // T
    return np.ascontiguousarray(xT.reshape(8, 128, ns, T).transpose(2, 1, 0, 3).reshape(ns, 128, 8 * T))


_prog_cache = {}


def _make_in_maps(x_prompt, x_sample, mem_prompt, mem_sample, w_in, q_norm, k_norm, rpb, w_mem_kv,
                  w_branch, w_out, ln_g, ln_b):
    x_prompt = np.asarray(x_prompt, np.float32)
    x_sample = np.asarray(x_sample, np.float32)
    mem_prompt = np.asarray(mem_prompt, np.float32)
    mem_sample = np.asarray(mem_sample, np.float32)
    Bp, Sp, _ = x_prompt.shape
    Bs, Ss, _ = x_sample.shape
    assert Bp == 2 and Bs == 8
    Pq = Sp // 4
    wt = _weight_tiles(np.asarray(w_in[0], np.float32), np.asarray(w_mem_kv[0], np.float32),
                       np.asarray(w_branch[0], np.float32), np.asarray(w_out[0], np.float32))
    biastab, cmask = _bias_tables(np.asarray(rpb[0], np.float32))
    qn = np.ascontiguousarray(np.tile(np.asarray(q_norm[0], np.float32)[None, :], (128, 1)))
    kn = np.ascontiguousarray(np.tile(np.asarray(k_norm[0], np.float32)[None, :], (128, 1)))
    lng = np.ascontiguousarray(np.tile(np.asarray(ln_g[0], np.float32)[None, :], (128, 1)))
    lnb = np.ascontiguousarray(np.tile(np.asarray(ln_b[0], np.float32)[None, :], (128, 1)))
    rope_s = _rope_table(np.arange(Ss))
    rope_p = _rope_table(np.arange(Sp))
    rv_s = _rv_table(Ss, 0, Ss)
    xkv_p = [np.ascontiguousarray(x_prompt[b].T) for b in range(Bp)]
    xkv_t = [_xt_tiles(x) for x in xkv_p]
    memT_p = [np.ascontiguousarray(mem_prompt[b].T) for b in range(Bp)]

    in_maps = []
    for c in range(8):
        m = dict(wt=wt, biastab=biastab, cmask=cmask, qn=qn, kn=kn, lng=lng, lnb=lnb)
        xs = x_sample[c]
        xq = np.zeros((D, Ss + 2 * T), np.float32)
        xq[:, T:T + Ss] = xs.T
        m['s_xq'] = _xt_tiles(xq)
        m['s_xtok'] = np.ascontiguousarray(xs)
        m['s_ropekv'] = rope_s
        m['s_ropeq'] = rope_s
        m['s_rv'] = rv_s
        m['s_memT'] = np.ascontiguousarray(mem_sample[c].T)
        b, qd = c // 4, c % 4
        q0 = qd * Pq
        xq = np.zeros((D, Pq + 2 * T), np.float32)
        lo, hi = max(q0 - T, 0), min(q0 + Pq + T, Sp)
        xq[:, lo - (q0 - T):hi - (q0 - T)] = xkv_p[b][:, lo:hi]
        m['p_xq'] = _xt_tiles(xq)
        m['p_xkv'] = xkv_t[b]
        m['p_xtok'] = np.ascontiguousarray(x_prompt[b, q0:q0 + Pq])
        m['p_ropekv'] = rope_p
        m['p_ropeq'] = np.ascontiguousarray(rope_p[q0:q0 + Pq])
        m['p_rv'] = _rv_table(Sp, q0, Pq)
        m['p_memT'] = memT_p[b]
        in_maps.append(m)
    return in_maps


def kernel(x_prompt, x_sample, mem_prompt, mem_sample, w_in, q_norm, k_norm, rpb, w_mem_kv,
           w_branch, w_out, ln_g, ln_b):
    Bp, Sp, _ = x_prompt.shape
    Bs, Ss, _ = x_sample.shape
    Pq = Sp // 4
    in_maps = _make_in_maps(x_prompt, x_sample, mem_prompt, mem_sample, w_in, q_norm, k_norm, rpb, w_mem_kv,
                            w_branch, w_out, ln_g, ln_b)
    key = (Ss, Sp)
    if key not in _prog_cache:
        _prog_cache[key] = build_program(Ss, Sp)[0]
    nc = _prog_cache[key]
    res = run_bass_kernel_spmd(nc, in_maps, core_ids=list(range(8)))
    y_prompt = np.zeros((Bp, Sp, D), np.float32)
    y_sample = np.zeros((Bs, Ss, D), np.float32)
    for c in range(8):
        r = res.results[c]
        y_sample[c] = r['s_y']
        b, qd = c // 4, c % 4
        y_prompt[b, qd * Pq:(qd + 1) * Pq] = r['p_y']
    return (y_prompt, y_sample)
```

```python
import types
import collections
import numpy as np
from contextlib import ExitStack
import concourse.bass as bass
import concourse.mybir as mybir
from concourse.bass_utils import run_bass_kernel_spmd

F32 = mybir.dt.float32
BF16 = mybir.dt.bfloat16
ALU = mybir.AluOpType
AF = mybir.ActivationFunctionType
AX = mybir.AxisListType

D = 1024
T = 256
NSLOT = 2
NPB = 3
NPAR = 4
NT = 29
TILE_EL = [4096] * 8 + [4096, 4096, 4096, 4096, 4096, 2048] * 2 + [16] * 4 + [4096, 4096] + [2048, 4096, 4096]
GRID_W = 64
DN_ALPHA = 2.0 ** 0.25
DN_BETA = 8.0 ** -0.25
EPS_QK = 1e-6
EPS_LN = 1e-5
OFFS = {0: [-2, -1, 0, 1, 2, 3], 1: [-2, -1, 0, 1, 2], 2: [-2, -1, 0, 1, 2], 3: [-2, -1, 0, 1, 2],
        4: [-3, -2, -1, 0, 1, 2]}


def blk_class(i, nl):
    if i == 0:
        return 0
    if i == 1:
        return 1
    if i == nl - 1:
        return 4
    if i == nl - 2:
        return 3
    return 2


def _freeze(fn):
    if fn.__closure__ is None:
        return fn
    cells = []
    for c in fn.__closure__:
        try:
            cells.append(types.CellType(c.cell_contents))
        except ValueError:
            cells.append(c)
    g = types.FunctionType(fn.__code__, fn.__globals__, fn.__name__, fn.__defaults__, tuple(cells))
    g.__kwdefaults__ = fn.__kwdefaults__
    return g


class _StopEmission(Exception):
    pass


DEBUG_STOP = [None]


def _phase(name):
    if DEBUG_STOP[0] is not None and name == DEBUG_STOP[0]:
        raise _StopEmission()


class Prog:
    def __init__(self):
        self.engs = {'pe': [], 'act': [], 'dve': [], 'pool': [], 'sp': []}
        self.cnt = {}
        self.waited = {e: {} for e in self.engs}
        self.lw = {}
        self.lr = {}
        self.const = set()
        self.n_inst = 0

    def _deps(self, reads, writes):
        d = {}

        def add(t):
            if t is None:
                return
            k, v = t
            if d.get(k, 0) < v:
                d[k] = v
        for k in reads:
            add(self.lw.get(k))
        for k in writes:
            add(self.lw.get(k))
            for kk, vv in self.lr.get(k, {}).items():
                add((kk, vv))
        return d

    def _filter(self, eng, d):
        out = []
        w = self.waited[eng]
        for k, v in d.items():
            if w.get(k, 0) < v:
                w[k] = v
                out.append((k, v))
        return out

    def _commit(self, ticket, reads, writes):
        k0, v0 = ticket
        for k in reads:
            if k in self.const:
                continue
            r = self.lr.setdefault(k, {})
            if r.get(k0, 0) < v0:
                r[k0] = v0
        for k in writes:
            self.lw[k] = ticket
            self.lr[k] = {}

    def emit(self, eng, fns, reads=(), writes=()):
        if callable(fns):
            fns = [fns]
        fns = [_freeze(f) for f in fns]
        d = self._deps(reads, writes)
        if eng == 'pe':
            d.pop(('eng', 'pe'), None)
        waits = self._filter(eng, d)
        key = ('eng', eng)
        self.cnt[key] = self.cnt.get(key, 0) + 1
        ticket = (key, self.cnt[key])
        self.engs[eng].append((waits, fns, key, 1))
        self.waited[eng][key] = max(self.waited[eng].get(key, 0), 0)
        self._commit(ticket, reads, writes)
        self.n_inst += len(fns)
        return ticket

    def dma(self, eng, out, in_, reads, writes, semkey):
        waits = self._filter(eng, self._deps(reads, writes))
        key = ('dma', semkey)
        self.cnt[key] = self.cnt.get(key, 0) + 16
        ticket = (key, self.cnt[key])
        self.engs[eng].append((waits, [lambda e, o=out, i=in_: e.dma_start(out=o, in_=i)], key, 16))
        self._commit(ticket, reads, writes)
        self.n_inst += 1
        return ticket

    def final_wait(self, eng):
        waits = self._filter(eng, dict(self.cnt))
        self.engs[eng].append((waits, None, None, 0))


def build_program(Ss, Sp):
    Pq = Sp // 4
    segs = [dict(n='s', Skv=Ss, nq=Ss, kv_in_q=True), dict(n='p', Skv=Sp, nq=Pq, kv_in_q=False)]
    SKV_MAX = max(Ss, Sp)
    nc = bass.Bass("TRN2", target_bir_lowering=False)
    P = Prog()

    def din(name, shape, dt=F32):
        return nc.dram_tensor(name, list(shape), dt, kind="ExternalInput").ap()

    dr = {}
    for sg in segs:
        n = sg['n']
        dr[n + '_xq'] = din(n + '_xq', [(sg['nq'] + 2 * T) // T, 128, 8 * T])
        if not sg['kv_in_q']:
            dr[n + '_xkv'] = din(n + '_xkv', [sg['Skv'] // T, 128, 8 * T])
        dr[n + '_xtok'] = din(n + '_xtok', [sg['nq'], D])
        dr[n + '_ropekv'] = din(n + '_ropekv', [sg['Skv'], 64])
        dr[n + '_ropeq'] = din(n + '_ropeq', [sg['nq'], 64])
        dr[n + '_rv'] = din(n + '_rv', [128, 70])
        dr[n + '_memT'] = din(n + '_memT', [D, 256])
        dr[n + '_y'] = nc.dram_tensor(n + '_y', [sg['nq'], D], F32, kind="ExternalOutput").ap()
    wt = din('wt', [NT, 128, 4096])
    biastab = din('biastab', [128, 7 * 8 * 128])
    cmask_d = din('cmask', [128, 7 * 128])
    qn_d = din('qn', [128, 64])
    kn_d = din('kn', [128, 64])
    lng_d = din('lng', [128, D])
    lnb_d = din('lnb', [128, D])
    wscr = nc.dram_tensor('wscr', [NT, 128, 4096], BF16, kind="Internal").ap()

    es = ExitStack()
    with es:
        sb_total = [0]

        def sb(name, shape, dt):
            nb = int(np.prod(shape[1:])) * (2 if dt == BF16 else 4)
            sb_total[0] += nb
            return es.enter_context(nc.sbuf_tensor(name, list(shape), dt))

        KA = sb('KA', [128, SKV_MAX], BF16)
        VA = sb('VA', [128, SKV_MAX // 128, 2, 65], BF16)
        XT = sb('XT', [128, 3, 8, T], BF16)
        WR = sb('WR', [128, NSLOT, 4096], BF16)
        QA = sb('QA', [128, 2, 4, 128], BF16)
        SZA2 = [sb('SZA%d' % k, [64, 8, T], BF16) for k in range(2)]
        QB = sb('QB', [128, 4, T], BF16)
        SZB2 = [sb('SZB%d' % k, [64, 8, T], BF16) for k in range(2)]
        QM = sb('QM', [128, 4, T], BF16)
        SZM2 = [sb('SZM%d' % k, [128, 4, T], BF16) for k in range(2)]
        KB = sb('KB', [128, 4, 3, T], BF16)
        VB = sb('VB', [128, 3, 2, 8, 65], BF16)
        KM = sb('KM', [128, 4, 256], BF16)
        VM = sb('VM', [128, 2, 512], BF16)
        MG = sb('MG', [128, 8, T], BF16)
        PB = sb('PB', [128, NPB, 1024], BF16)
        ET = sb('ET', [128, 7, 8, 128], BF16)
        RV = sb('RV', [128, 2, 70], F32)
        XTOK = sb('XTOK', [128, D], F32)
        YB = sb('YB', [128, 2, D], F32)
        OSB = sb('OSB', [128, 1024], F32)
        RZ = sb('RZ', [128, 1024], F32)
        LNG = sb('LNG', [128, D], F32)
        LNB = sb('LNB', [128, D], F32)
        QN = sb('QN', [128, 64], F32)
        KN = sb('KN', [128, 64], F32)
        RT = sb('RT', [128, 2, 2, 64], F32)
        GT = sb('GT', [128, NPAR, 4, 32], F32)
        XS = sb('XS', [128, 512], F32)
        TMP = sb('TMP', [128, 2, 256], F32)
        ORO = sb('ORO', [128, 512], F32)
        QR = sb('QR', [128, 512], BF16)
        SS = sb('SS', [128, 24], F32)
        ST = sb('ST', [128, 2, 8], F32)
        SGT = sb('SGT', [128, 2, 512], BF16)
        MT = sb('MT', [128, 512], F32)
        IDENTF = sb('IDENTF', [128, 128], F32)
        IDENT = sb('IDENT', [128, 128], BF16)
        ONESF = sb('ONESF', [128, 64], F32)
        ONESB = sb('ONESB', [128, 128], BF16)
        EPS = sb('EPS', [128, 2], F32)
        PS = es.enter_context(nc.psum_tensor('PS', [128, 4096], F32))
        assert sb_total[0] <= 210000, sb_total[0]
        WKVA = MG

        P.const |= {'IDENT', 'ONESF', 'ONESB', 'EPS', 'VA1', 'VB1', 'QN', 'KN', 'LNG', 'LNB', 'ET'}
        xbank_ctr = [0]

        def next_bank():
            b = [6, 7, 4, 5, 0, 1, 2, 3][xbank_ctr[0] % 8]
            xbank_ctr[0] += 1
            return b

        P.emit('pool', lambda e: e.memset(VA[:, :, :, 64:65], 1.0), writes=['VA1'])
        P.emit('pool', lambda e: e.memset(VB[:, :, :, :, 64:65], 1.0), writes=['VB1'])
        P.emit('pool', lambda e: e.memset(ONESF[:], 1.0), writes=['ONESF'])
        P.emit('pool', lambda e: e.memset(ONESB[:], 1.0), writes=['ONESB'])
        P.emit('pool', lambda e: e.memset(EPS[:, 0:1], EPS_QK), writes=['EPS0'])
        P.emit('pool', lambda e: e.memset(EPS[:, 1:2], EPS_LN), reads=['EPS0'], writes=['EPS'])
        P.emit('pool', lambda e: e.memset(IDENTF[:], 0.0), writes=['IDENTF'])
        P.emit('pool', lambda e: e.affine_select(out=IDENTF[:], in_=IDENTF[:], pattern=[[-1, 128]],
                                                 compare_op=ALU.not_equal, fill=1.0, base=0,
                                                 channel_multiplier=1), reads=['IDENTF'], writes=['IDENTF'])
        P.emit('pool', lambda e: e.tensor_copy(out=IDENT[:], in_=IDENTF[:]), reads=['IDENTF'], writes=['IDENT'])
        P.dma('sp', QN[:], qn_d[:], [], ['QN'], 'c0')
        P.dma('sp', KN[:], kn_d[:], [], ['KN'], 'c1')
        P.dma('sp', LNG[:], lng_d[:], [], ['LNG'], 'c2')
        P.dma('sp', LNB[:], lnb_d[:], [], ['LNB'], 'c3')
        for si, sg in enumerate(segs):
            P.dma('sp', RV[:, si, :], dr[sg['n'] + '_rv'][:], [], [('RV', si)], ('c4', si))
            P.const.add(('RV', si))
        P.dma('sp', RZ[:, 0:896], cmask_d[:], [], ['RZ'], 'c6')
        for oi in range(7):
            P.dma('sp', OSB[:], biastab[:, oi * 1024:(oi + 1) * 1024], [], ['OSB'], 'c7')
            P.emit('act', lambda e: e.activation(out=OSB[:], in_=OSB[:], func=AF.Exp), reads=['OSB'], writes=['OSB'])
            P.emit('dve', lambda e: e.tensor_tensor(
                out=ET[:, oi, :, :], in0=OSB[:].rearrange("p (h q) -> p h q", h=8),
                in1=RZ[:, oi * 128:(oi + 1) * 128].unsqueeze(1).broadcast_to([128, 8, 128]), op=ALU.mult),
                reads=['OSB', 'RZ'], writes=[('ETw', oi)])
        P.emit('dve', lambda e: e.tensor_copy(out=SS[:, 0:1], in_=EPS[:, 0:1]),
               reads=[('ETw', oi) for oi in range(7)] + ['EPS'], writes=['ET'] + [('SS', q) for q in range(NPAR)])

        wr_g = [0]

        def wload(ti):
            slot = wr_g[0] % NSLOT
            wr_g[0] += 1
            nel = TILE_EL[ti]
            P.dma('sp', WR[:, slot, 0:nel], wscr[ti, :, 0:nel], [('wscr', ti)], [('WR', slot)], ('wr', slot))
            return slot

        conv_next = [0]
        conv_order = [27, 28] + list(range(0, 20)) + [24, 25]

        def convert_one():
            if conv_next[0] >= len(conv_order):
                return
            ti = conv_order[conv_next[0]]
            conv_next[0] += 1
            slot = wr_g[0] % NSLOT
            wr_g[0] += 1
            nel = TILE_EL[ti]
            P.dma('pool', WR[:, slot, 0:nel], wt[ti, :, 0:nel], [], [('WR', slot)], ('wrc', slot))
            P.dma('sp', wscr[ti, :, 0:nel], WR[:, slot, 0:nel], [('WR', slot)], [('wscr', ti)], ('ws', slot))

        xt_g = [0]

        def xt_load(src, col0):
            slot = xt_g[0] % 3
            xt_g[0] += 1
            P.dma('pool', XT[:, slot].rearrange("p c t -> p (c t)"), src[col0 // T], [], [('XT', slot)], ('xt', slot))
            return slot

        def rope_chain(H, gains_key, gains, rt_key, rt_ap, dst_keys, dst_ap, par=0, full=True):
            W = H * 32
            xo = 0 if full else par * 128
            to = 0 if full else par * 64
            sb0 = 0 if full else par * 2

            def K(name, *extra):
                if full:
                    return [(name,) + extra + (q,) for q in range(NPAR)]
                return [(name,) + extra + (par,)]
            xs_ap = XS[:, xo:xo + H * 64]
            cos = rt_ap[:, 0:32].rearrange("p (a f) -> p a f", a=2)
            sin = rt_ap[:, 32:64].rearrange("p (a f) -> p a f", a=2)
            g4 = gains[:].rearrange("p (a j f) -> p a j f", a=2, j=2)
            g1 = g4[:, :, 0, :]
            g2 = g4[:, :, 1, :]
            gp = 0 if full else par
            gt = [GT[:, gp, k, :].rearrange("p (a f) -> p a f", a=2) for k in range(4)]
            gk = [[('GT', k, gp)] for k in range(4)]
            P.emit('pool', lambda e: e.tensor_tensor(out=gt[0], in0=cos, in1=g1, op=ALU.mult), reads=[rt_key, gains_key], writes=gk[0])
            P.emit('pool', lambda e: e.tensor_tensor(out=gt[1], in0=sin, in1=g2, op=ALU.mult), reads=[rt_key, gains_key], writes=gk[1])
            P.emit('pool', lambda e: e.tensor_tensor(out=gt[2], in0=cos, in1=g2, op=ALU.mult), reads=[rt_key, gains_key], writes=gk[2])
            P.emit('pool', lambda e: e.tensor_tensor(out=gt[3], in0=sin, in1=g1, op=ALU.mult), reads=[rt_key, gains_key], writes=gk[3])
            x5 = xs_ap.rearrange("p (h a j f) -> p h a j f", h=H, a=2, j=2)
            x1 = x5[:, :, :, 0, :]
            x2 = x5[:, :, :, 1, :]
            oro = ORO[:, xo:xo + H * 64]
            o5 = oro.rearrange("p (h a j f) -> p h a j f", h=H, a=2, j=2)
            tm = [TMP[:, k, to:to + W].rearrange("p (h a f) -> p h a f", h=H, a=2) for k in range(2)]
            bcg = [g.unsqueeze(1).broadcast_to([128, H, 2, 16]) for g in gt]
            ssv = SS[:, sb0:sb0 + H]
            sdv = SS[:, 8 + sb0:8 + sb0 + H]
            rsv = SS[:, 16 + sb0:16 + sb0 + H]
            P.emit('dve', lambda e: e.tensor_tensor(out=oro, in0=xs_ap, in1=xs_ap, op=ALU.mult), reads=K('XS'), writes=K('ORO'))
            P.emit('dve', lambda e: e.tensor_reduce(out=ssv, in_=oro.rearrange("p (h d) -> p h d", h=H), axis=AX.X, op=ALU.add),
                   reads=K('ORO'), writes=K('SS'))
            P.emit('act', lambda e: e.activation(out=sdv, in_=ssv, func=AF.Sqrt, bias=EPS[:, 0:1], scale=1.0 / 64),
                   reads=K('SS') + ['EPS'], writes=K('SS'))
            P.emit('dve', lambda e: e.tensor_tensor(out=tm[0], in0=x1, in1=bcg[0], op=ALU.mult), reads=K('XS') + gk[0], writes=K('TMP', 0))
            P.emit('dve', lambda e: e.tensor_tensor(out=tm[1], in0=x2, in1=bcg[1], op=ALU.mult), reads=K('XS') + gk[1], writes=K('TMP', 1))
            P.emit('dve', lambda e: e.tensor_tensor(out=o5[:, :, :, 0, :], in0=tm[0], in1=tm[1], op=ALU.subtract),
                   reads=K('TMP', 0) + K('TMP', 1), writes=K('ORO'))
            P.emit('dve', lambda e: e.tensor_tensor(out=tm[0], in0=x2, in1=bcg[2], op=ALU.mult), reads=K('XS') + gk[2], writes=K('TMP', 0))
            P.emit('dve', lambda e: e.tensor_tensor(out=tm[1], in0=x1, in1=bcg[3], op=ALU.mult), reads=K('XS') + gk[3], writes=K('TMP', 1))
            P.emit('dve', lambda e: e.tensor_tensor(out=o5[:, :, :, 1, :], in0=tm[0], in1=tm[1], op=ALU.add),
                   reads=K('TMP', 0) + K('TMP', 1), writes=K('ORO'))
            P.emit('dve', lambda e: e.reciprocal(out=rsv, in_=sdv), reads=K('SS'), writes=K('SS'))
            P.emit('dve', lambda e: e.tensor_tensor(
                out=dst_ap.rearrange("p (h d) -> p h d", h=H), in0=oro.rearrange("p (h d) -> p h d", h=H),
                in1=rsv.unsqueeze(2).broadcast_to([128, H, 64]), op=ALU.mult),
                reads=K('ORO') + K('SS'), writes=dst_keys)

        pending_ng = [None]

        def ng_flush():
            if pending_ng[0] is not None:
                f = pending_ng[0]
                pending_ng[0] = None
                f()

        def norm_gate(sz_view, sz_keys):
            ng_flush()
            P.emit('dve', lambda e: e.tensor_copy(out=OSB[0:65, :], in_=PS[0:65, 2048:3072]),
                   reads=[('ps', 4), ('ps', 5)], writes=['OSB'])
            P.emit('dve', lambda e: e.reciprocal(out=OSB[64:65, :], in_=OSB[64:65, :]), reads=['OSB'], writes=['OSB'])

            def part1():
                P.emit('pe', [lambda e: e.matmul(PS[0:64, 3072:3584], lhsT=ONESF[64:65, 0:64], rhs=OSB[64:65, 0:512], start=True, stop=True),
                              lambda e: e.matmul(PS[0:64, 3584:4096], lhsT=ONESF[64:65, 0:64], rhs=OSB[64:65, 512:1024], start=True, stop=True)],
                       reads=['OSB', 'ONESF'], writes=[('ps', 6), ('ps', 7)])
                P.emit('dve', lambda e: e.tensor_tensor(out=RZ[0:64, :].rearrange("p (h q) -> p h q", h=8),
                                                        in0=PS[0:64, 3072:4096].rearrange("p (h q) -> p h q", h=8),
                                                        in1=sz_view, op=ALU.mult),
                       reads=[('ps', 6), ('ps', 7)] + sz_keys, writes=['RZ'])
                P.emit('dve', lambda e: e.tensor_tensor(out=sz_view, in0=OSB[0:64, :].rearrange("p (h q) -> p h q", h=8),
                                                        in1=RZ[0:64, :].rearrange("p (h q) -> p h q", h=8), op=ALU.mult),
                       reads=['OSB', 'RZ'], writes=sz_keys)
            pending_ng[0] = part1

        pb_g = [0]

        def next_pb():
            s = pb_g[0] % NPB
            pb_g[0] += 1
            return s

        jobs = collections.deque()
        mid_job = [False]

        def pop_job():
            f = jobs.popleft()
            f()
            mid_job[0] = getattr(f, 'half', 0) == 1
        try:
          for si, sg in enumerate(segs):
              n = sg['n']
              Skv, nq = sg['Skv'], sg['nq']
              nsteps = nq // T
              nl = nq // 128
              nkt = Skv // 128
              xq = dr[n + '_xq']
              if sg['kv_in_q']:
                  kvsrc, kvoff = xq, T
              else:
                  kvsrc, kvoff = dr[n + '_xkv'], 0
              ropekv = dr[n + '_ropekv'].rearrange("(c p) f -> p c f", p=128)
              ropeq = dr[n + '_ropeq'].rearrange("(c p) f -> p c f", p=128)
              xtok = dr[n + '_xtok']
              ydr = dr[n + '_y']

              _phase('phase1')
              P.dma('pool', WKVA[:].rearrange("p c n -> p (c n)"), wt[26, :, 0:2048], [], [('MG', c) for c in range(8)], 'c5')
              mgk = [('MG', c) for c in range(8)]
              for ci in range(Skv // T):
                  if si == 0:
                      convert_one()
                  xs_slot = xt_load(kvsrc, kvoff + ci * T)
                  rb = ci % 2
                  P.dma('sp', RT[:, rb], ropekv[:, ci * 2:ci * 2 + 2, :], [], [('RT', rb)], ('rt', rb))
                  for tt in range(2):
                      kt = ci * 2 + tt
                      b = next_bank()
                      P.emit('pe', [lambda e, c=c: e.matmul(
                          PS[:, b * 512:b * 512 + 256], lhsT=XT[:, xs_slot, c, tt * 128:(tt + 1) * 128], rhs=WKVA[:, c, :],
                          start=(c == 0), stop=(c == 7)) for c in range(8)],
                          reads=[('XT', xs_slot)] + mgk, writes=[('ps', b)])
                      kp = kt % NPAR
                      P.emit('act', lambda e: e.activation(out=XS[:, kp * 128:kp * 128 + 128], in_=PS[:, b * 512:b * 512 + 128], func=AF.Copy),
                             reads=[('ps', b)], writes=[('XS', kp)])
                      P.emit('act', lambda e: e.activation(
                          out=VA[:, kt, :, 0:64], in_=PS[:, b * 512 + 128:b * 512 + 256].rearrange("p (k d) -> p k d", k=2), func=AF.Copy),
                          reads=[('ps', b)], writes=[('VA', kt)])
                      rope_chain(2, 'KN', KN, ('RT', rb), RT[:, rb, tt, :], [('QR', kp)], QR[:, kp * 128:kp * 128 + 128], par=kp, full=False)
                      b2 = next_bank()
                      P.emit('pe', lambda e: e.transpose(
                          PS[:, b2 * 512:b2 * 512 + 64].bitcast(BF16), QR[:, kp * 128:kp * 128 + 128], IDENT[:]),
                          reads=[('QR', kp), 'IDENT'], writes=[('ps', b2)])
                      P.emit('act', lambda e: e.activation(
                          out=KA[:, kt * 128:(kt + 1) * 128], in_=PS[:, b2 * 512:b2 * 512 + 64].bitcast(BF16), func=AF.Copy),
                          reads=[('ps', b2)], writes=[('KA', kt)])
              if si == 0:
                  while conv_next[0] < len(conv_order):
                      convert_one()

              _phase('memkv')
              ms = xt_g[0] % 3
              xt_g[0] += 1
              P.dma('pool', XT[:, ms], dr[n + '_memT'].rearrange("(c p) t -> p c t", p=128), [], [('XT', ms)], ('xt', ms))
              ws = wload(27)
              for h in range(4):
                  b = next_bank()
                  P.emit('pe', [lambda e, c=c: e.matmul(
                      PS[:, b * 512:b * 512 + 256], lhsT=WR[:, ws, c * 512 + h * 128:c * 512 + (h + 1) * 128], rhs=XT[:, ms, c, :],
                      start=(c == 0), stop=(c == 7)) for c in range(8)],
                      reads=[('XT', ms), ('WR', ws)], writes=[('ps', b)])
                  P.emit('dve', lambda e: e.tensor_copy(out=KM[:, h, :], in_=PS[:, b * 512:b * 512 + 256]),
                         reads=[('ps', b)], writes=[('KM', h)])
              ws = wload(28)
              for mt in range(2):
                  b = next_bank()
                  P.emit('pe', [lambda e, c=c: e.matmul(
                      PS[:, b * 512:(b + 1) * 512], lhsT=XT[:, ms, c, mt * 128:(mt + 1) * 128], rhs=WR[:, ws, c * 512:(c + 1) * 512],
                      start=(c == 0), stop=(c == 7)) for c in range(8)],
                      reads=[('XT', ms), ('WR', ws)], writes=[('ps', b)])
                  P.emit('dve', lambda e: e.tensor_copy(out=VM[:, mt, :], in_=PS[:, b * 512:(b + 1) * 512]),
                         reads=[('ps', b)], writes=[('VM', mt)])

              _phase('kvb')
              def kvb(jj, xs_):
                  rs = (jj + 1) % 3
                  ws = wload(0)
                  for p in range(4):
                      b = next_bank()
                      P.emit('pe', [lambda e, c=c: e.matmul(
                          PS[:, b * 512:b * 512 + T], lhsT=WR[:, ws, c * 512 + p * 128:c * 512 + (p + 1) * 128], rhs=XT[:, xs_, c, :],
                          start=(c == 0), stop=(c == 7)) for c in range(8)],
                          reads=[('XT', xs_), ('WR', ws)], writes=[('ps', b)])
                      P.emit('dve', lambda e: e.tensor_copy(out=KB[:, p, rs, :], in_=PS[:, b * 512:b * 512 + T]),
                             reads=[('ps', b)], writes=[('KB', rs, p)])
                  ws = wload(1)
                  for tt in range(2):
                      b = next_bank()
                      P.emit('pe', [lambda e, c=c: e.matmul(
                          PS[:, b * 512:(b + 1) * 512], lhsT=XT[:, xs_, c, tt * 128:(tt + 1) * 128], rhs=WR[:, ws, c * 512:(c + 1) * 512],
                          start=(c == 0), stop=(c == 7)) for c in range(8)],
                          reads=[('XT', xs_), ('WR', ws)], writes=[('ps', b)])
                      P.emit('dve', lambda e: e.tensor_copy(
                          out=VB[:, rs, tt, :, 0:64], in_=PS[:, b * 512:(b + 1) * 512].rearrange("p (h d) -> p h d", h=8)),
                          reads=[('ps', b)], writes=[('VB', rs, tt)])

              def fm_proj(ti, ngroups, M, xs_, evac):
                  ws = wload(ti)
                  for g in range(ngroups):
                      b = next_bank()
                      P.emit('pe', [lambda e, c=c: e.matmul(
                          PS[0:M, b * 512:b * 512 + T], lhsT=WR[:, ws, c * 512 + g * M:c * 512 + (g + 1) * M], rhs=XT[:, xs_, c, :],
                          start=(c == 0), stop=(c == 7)) for c in range(8)],
                          reads=[('XT', xs_), ('WR', ws)], writes=[('ps', b)])
                      evac(g, b)

              def fm_proj_zpair(ti, xs_, dst, keyname):
                ws = wload(ti)
                for p in range(4):
                    b = next_bank()
                    P.emit('pe', [lambda e, c=c: e.matmul(
                        PS[:, b * 512:b * 512 + T], lhsT=WR[:, ws, c * 512 + p * 128:c * 512 + (p + 1) * 128], rhs=XT[:, xs_, c, :],
                        start=(c == 0), stop=(c == 7)) for c in range(8)],
                        reads=[('XT', xs_), ('WR', ws)], writes=[('ps', b)])
                    P.emit('act', lambda e: e.activation(out=dst[:, 2 * p, :], in_=PS[0:64, b * 512:b * 512 + T], func=AF.Silu),
                           reads=[('ps', b)], writes=[(keyname, 2 * p)])
                    P.emit('act', lambda e: e.activation(out=dst[:, 2 * p + 1, :], in_=PS[64:128, b * 512:b * 512 + T], func=AF.Silu),
                           reads=[('ps', b)], writes=[(keyname, 2 * p + 1)])

              def ev_silu(dst, keyname, M):
                  def f(g, b):
                      P.emit('act', lambda e: e.activation(out=dst[:, g, :], in_=PS[0:M, b * 512:b * 512 + T], func=AF.Silu),
                             reads=[('ps', b)], writes=[(keyname, g)])
                  return f

              def ev_copy(dst, keyname):
                  def f(g, b):
                      P.emit('dve', lambda e: e.tensor_copy(out=dst[:, g, :], in_=PS[:, b * 512:b * 512 + T]),
                             reads=[('ps', b)], writes=[(keyname, g)])
                  return f

              xslot = {}
              xslot[-1] = xt_load(xq, 0)
              xslot[0] = xt_load(xq, T)
              kvb(-1, xslot[-1])
              kvb(0, xslot[0])
              for j in range(nsteps):
                  xslot[j + 1] = xt_load(xq, (j + 2) * T)
                  xc = xslot[j]
                  kvb(j + 1, xslot[j + 1])
                  _phase('proj')
                  rb = j % 2
                  P.dma('sp', RT[:, rb], ropeq[:, j * 2:j * 2 + 2, :], [], [('RT', rb)], ('rt', rb))
                  ws = wload(2)
                  for u in range(2):
                      b = next_bank()
                      P.emit('pe', [lambda e, c=c: e.matmul(
                          PS[:, b * 512:(b + 1) * 512], lhsT=XT[:, xc, c, u * 128:(u + 1) * 128], rhs=WR[:, ws, c * 512:(c + 1) * 512],
                          start=(c == 0), stop=(c == 7)) for c in range(8)],
                          reads=[('XT', xc), ('WR', ws)], writes=[('ps', b)])
                      P.emit('act', lambda e: e.activation(out=XS[:], in_=PS[:, b * 512:(b + 1) * 512], func=AF.Copy),
                             reads=[('ps', b)], writes=[('XS', q) for q in range(NPAR)])
                      rope_chain(8, 'QN', QN, ('RT', rb), RT[:, rb, u, :], [('QR', q) for q in range(NPAR)], QR[:])
                      b2 = next_bank()
                      P.emit('pe', [lambda e, s=s: e.transpose(
                          PS[:, b2 * 512 + s * 64:b2 * 512 + (s + 1) * 64].bitcast(BF16), QR[:, s * 128:(s + 1) * 128], IDENT[:])
                          for s in range(4)], reads=[('QR', q) for q in range(NPAR)] + ['IDENT'], writes=[('ps', b2)])
                      P.emit('dve', lambda e: e.tensor_copy(
                          out=QA[:, u, :, :].rearrange("p s q -> p (s q)"), in_=PS[:, b2 * 512:b2 * 512 + 256].bitcast(BF16)),
                          reads=[('ps', b2)], writes=[('QA', u)])
                  sp_ = j % 2
                  SZA, SZB, SZM = SZA2[sp_], SZB2[sp_], SZM2[sp_]
                  ng_flush()
                  fm_proj_zpair(3, xc, SZA, ('SZA', sp_))
                  fm_proj(4, 4, 128, xc, ev_copy(QB, 'QB'))
                  fm_proj_zpair(5, xc, SZB, ('SZB', sp_))
                  fm_proj(6, 4, 128, xc, ev_copy(QM, 'QM'))
                  fm_proj(7, 4, 128, xc, ev_silu(SZM, ('SZM', sp_), 128))

                  sza_keys = [(('SZA', sp_), g) for g in range(8)]
                  szb_keys = [(('SZB', sp_), g) for g in range(8)]
                  szm_keys = [(('SZM', sp_), g) for g in range(4)]
                  _phase('attB')
                  for sub in range(2):
                      i = 2 * j + sub
                      cls = blk_class(i, nl)
                      offs = OFFS[cls]
                      def b_geom(oidx):
                          kb_ = i + offs[oidx]
                          return (kb_ // 2 + 1) % 3, kb_ % 2

                      def b_qk(oidx):
                          rs, ksub = b_geom(oidx)
                          sb_ = oidx % 2
                          P.emit('pe', [lambda e, h=h: e.matmul(
                              PS[:, sb_ * 1024 + (h % 2) * 512 + (h // 2) * 128:sb_ * 1024 + (h % 2) * 512 + (h // 2 + 1) * 128],
                              lhsT=KB[(h % 2) * 64:(h % 2) * 64 + 64, h // 2, rs, ksub * 128:(ksub + 1) * 128],
                              rhs=QB[(h % 2) * 64:(h % 2) * 64 + 64, h // 2, sub * 128:(sub + 1) * 128], start=True, stop=True)
                              for h in range(8)],
                              reads=[('KB', rs, p) for p in range(4)] + [('QB', p) for p in range(4)],
                              writes=[('ps', 2 * sb_), ('ps', 2 * sb_ + 1)])

                      def b_rest(oidx):
                          rs, ksub = b_geom(oidx)
                          sb_ = oidx % 2
                          o = offs[oidx]
                          p0 = next_pb()
                          p1 = next_pb()
                          P.emit('act', lambda e: e.activation(out=PB[:, p0, :], in_=PS[:, sb_ * 1024:(sb_ + 1) * 1024], func=AF.Exp, scale=0.125),
                                 reads=[('ps', 2 * sb_), ('ps', 2 * sb_ + 1)], writes=[('PB', p0)])
                          oi = o + 3
                          for a in range(2):
                              col = cls * 14 + oi * 2 + a
                              P.emit('dve', lambda e: e.scalar_tensor_tensor(
                                  out=PB[:, p1, :].rearrange("p (h q) -> p h q", h=8)[:, :, a * 64:(a + 1) * 64],
                                  in0=PB[:, p0, :].rearrange("p (h q) -> p h q", h=8)[:, :, a * 64:(a + 1) * 64],
                                  scalar=RV[:, si, col:col + 1], in1=ET[:, oi, :, a * 64:(a + 1) * 64],
                                  op0=ALU.mult, op1=ALU.mult),
                                  reads=[('PB', p0), ('RV', si), 'ET'], writes=[('PB', p1)])
                          P.emit('pe', [lambda e, h=h: e.matmul(
                              PS[0:65, 2048 + h * 128:2048 + (h + 1) * 128], lhsT=VB[:, rs, ksub, h, :],
                              rhs=PB[:, p1, ((h % 2) * 4 + h // 2) * 128:((h % 2) * 4 + h // 2 + 1) * 128],
                              start=(oidx == 0 and h % 4 == 0), stop=(oidx == len(offs) - 1),
                              skip_group_check=True)
                              for h in range(8)],
                              reads=[('VB', rs, ksub), 'VB1', ('PB', p1)], writes=[('ps', 4), ('ps', 5)])
                      b_qk(0)
                      b_qk(1)
                      for oidx in range(len(offs)):
                          if oidx == 2:
                              ng_flush()
                          b_rest(oidx)
                          if oidx + 2 < len(offs):
                              b_qk(oidx + 2)
                      norm_gate(SZB[:, :, sub * 128:(sub + 1) * 128], szb_keys)

                  _phase('attM')
                  ng_flush()
                  for h in range(4):
                      sb_ = h % 2
                      P.emit('pe', [lambda e, mt=mt: e.matmul(
                          PS[:, sb_ * 1024 + mt * T:sb_ * 1024 + (mt + 1) * T], lhsT=KM[:, h, mt * 128:(mt + 1) * 128], rhs=QM[:, h, :],
                          start=True, stop=True) for mt in range(2)],
                          reads=[('KM', h), ('QM', h)], writes=[('ps', 2 * sb_)])
                      pbs = next_pb()
                      P.emit('act', lambda e: e.activation(out=PB[:, pbs, 0:512], in_=PS[:, sb_ * 1024:sb_ * 1024 + 512], func=AF.Exp,
                                                           scale=float(128 ** -0.5)),
                             reads=[('ps', 2 * sb_)], writes=[('PB', pbs)])
                      ob = 4 + h // 2
                      xb = 6 + h // 2
                      oc = (h % 2) * T
                      P.emit('pe', [lambda e, mt=mt: e.matmul(
                          PS[:, ob * 512 + oc:ob * 512 + oc + T], lhsT=VM[:, mt, h * 128:(h + 1) * 128], rhs=PB[:, pbs, mt * T:(mt + 1) * T],
                          start=(mt == 0), stop=(mt == 1)) for mt in range(2)] + [lambda e, mt=mt: e.matmul(
                              PS[:, xb * 512 + oc:xb * 512 + oc + T], lhsT=ONESB[:], rhs=PB[:, pbs, mt * T:(mt + 1) * T],
                              start=(mt == 0), stop=(mt == 1)) for mt in range(2)],
                          reads=[('VM', 0), ('VM', 1), ('PB', pbs), 'ONESB'], writes=[('ps', ob), ('ps', xb)])
                  P.emit('dve', lambda e: e.reciprocal(out=OSB[:], in_=PS[:, 3072:4096]),
                         reads=[('ps', 6), ('ps', 7)], writes=['OSB'])
                  P.emit('dve', lambda e: e.tensor_tensor(out=RZ[:], in0=OSB[:], in1=SZM[:].rearrange("p h t -> p (h t)"), op=ALU.mult),
                         reads=['OSB'] + szm_keys, writes=['RZ'])
                  P.emit('dve', lambda e: e.tensor_tensor(out=SZM[:].rearrange("p h t -> p (h t)"), in0=PS[:, 2048:3072], in1=RZ[:], op=ALU.mult),
                         reads=['RZ', ('ps', 4), ('ps', 5)], writes=szm_keys)

                  _phase('attA')
                  pop_stride = 3 if 48 <= nkt < 128 else 2
                  for u in range(2):
                      def qk(kt):
                          sb_ = kt % 2
                          P.emit('pe', [
                              lambda e: e.matmul(PS[:, sb_ * 1024:sb_ * 1024 + 512], lhsT=KA[0:64, kt * 128:(kt + 1) * 128],
                                                 rhs=QA[0:64, u, :, :].rearrange("p s q -> p (s q)"), start=True, stop=True),
                              lambda e: e.matmul(PS[:, sb_ * 1024 + 512:sb_ * 1024 + 1024], lhsT=KA[64:128, kt * 128:(kt + 1) * 128],
                                                 rhs=QA[64:128, u, :, :].rearrange("p s q -> p (s q)"), start=True, stop=True)],
                              reads=[('KA', kt), ('QA', u)], writes=[('ps', 2 * sb_), ('ps', 2 * sb_ + 1)])

                      def ex_pv(kt):
                          sb_ = kt % 2
                          pbs = next_pb()
                          P.emit('act', lambda e: e.activation(out=PB[:, pbs, :], in_=PS[:, sb_ * 1024:(sb_ + 1) * 1024], func=AF.Exp, scale=0.125),
                                 reads=[('ps', 2 * sb_), ('ps', 2 * sb_ + 1)], writes=[('PB', pbs)])
                          P.emit('pe', [
                              lambda e: e.matmul(PS[0:65, 2048:2560], lhsT=VA[:, kt, 0, :], rhs=PB[:, pbs, 0:512],
                                                 start=(kt == 0), stop=(kt == nkt - 1)),
                              lambda e: e.matmul(PS[0:65, 2560:3072], lhsT=VA[:, kt, 1, :], rhs=PB[:, pbs, 512:1024],
                                                 start=(kt == 0), stop=(kt == nkt - 1))],
                              reads=[('VA', kt), 'VA1', ('PB', pbs)], writes=[('ps', 4), ('ps', 5)])
                      qk(0)
                      qk(1)
                      for kt in range(nkt):
                          if kt == min(8, nkt - 1):
                              if mid_job[0] and jobs:
                                  pop_job()
                              ng_flush()
                          if kt % pop_stride == 1 and jobs:
                              pop_job()
                          ex_pv(kt)
                          if kt + 2 < nkt:
                              qk(kt + 2)
                      norm_gate(SZA[:, :, u * 128:(u + 1) * 128], sza_keys)

                  _phase('merge')
                  while jobs:
                      pop_job()

                  def build_jobs(j=j, xc=xc, SZA=SZA, SZB=SZB, SZM=SZM, sza_keys=sza_keys, szb_keys=szb_keys,
                                 szm_keys=szm_keys, n=n, ydr=ydr, xtok=xtok):
                      out = []
                      jk = [0]
                      order = [(tt, ct, nb_) for tt in range(2) for ct in range(2) for nb_ in (2, 1, 0)]
                      NO = len(order)
                      allslots = [dict() for _ in range(NO + 1)]
                      MBv = [XS[:].bitcast(BF16), ORO[:].bitcast(BF16)]
                      mbk = [[('XS', q) for q in range(NPAR)], [('ORO', q) for q in range(NPAR)]]

                      def ensure(g_, which):
                          sl = allslots[g_]
                          if which not in sl:
                              if g_ < NO:
                                  _, ct, nb_ = order[g_]
                                  sl[which] = wload(8 + ct * 6 + nb_ * 2 + (0 if which == 'wg' else 1))
                              else:
                                  sl[which] = wload(24 if which == 'wg' else 25)
                      for g_, (tt, ct, nb_) in enumerate(order):
                          slots = allslots[g_]
                          idx = g_ % 3
                          if True:
                              def p1(g_=g_, ct=ct, nb_=nb_, slots=slots, tt=tt):
                                  ensure(g_, 'wg')
                                  ensure(g_, 'wb')
                                  wg = slots['wg']
                                  P.emit('pe', [lambda e, c=c: e.matmul(
                                      PS[:, 3072:3584], lhsT=XT[:, xc, c, tt * 128:(tt + 1) * 128], rhs=WR[:, wg, c * 512:(c + 1) * 512],
                                      start=(c == 0), stop=(c == 7)) for c in range(8)],
                                      reads=[('XT', xc), ('WR', wg)], writes=[('ps', 6)])
                                  ensure(g_ + 1, 'wg')

                              def p2(g_=g_, ct=ct, nb_=nb_, slots=slots, tt=tt, idx=idx):
                                  wb = slots['wb']
                                  k = jk[0] % 2
                                  jk[0] += 1
                                  if nb_ == 0:
                                      P.emit('pe', [lambda e, h=h: e.matmul(
                                          PS[:, 3584:4096], lhsT=SZA[:, h, tt * 128:(tt + 1) * 128], rhs=WR[0:64, wb, h * 512:(h + 1) * 512],
                                          start=(h == 0), stop=(h == 7)) for h in range(8)],
                                          reads=[('WR', wb)] + sza_keys, writes=[('ps', 7)])
                                  elif nb_ == 1:
                                      P.emit('pe', [lambda e, h=h: e.matmul(
                                          PS[:, 3584:4096], lhsT=SZB[:, h, tt * 128:(tt + 1) * 128], rhs=WR[0:64, wb, h * 512:(h + 1) * 512],
                                          start=(h == 0), stop=(h == 7)) for h in range(8)],
                                          reads=[('WR', wb)] + szb_keys, writes=[('ps', 7)])
                                  else:
                                      P.emit('pe', [lambda e, h=h: e.matmul(
                                          PS[:, 3584:4096], lhsT=SZM[:, h, tt * 128:(tt + 1) * 128], rhs=WR[:, wb, h * 512:(h + 1) * 512],
                                          start=(h == 0), stop=(h == 3)) for h in range(4)],
                                          reads=[('WR', wb)] + szm_keys, writes=[('ps', 7)])
                                  ensure(g_ + 1, 'wb')
                                  P.emit('act', lambda e: e.activation(out=SGT[:, k, :], in_=PS[:, 3072:3584], func=AF.Tanh, scale=0.5),
                                         reads=[('ps', 6)], writes=[('SGT', k)])
                                  acc = YB[:, tt, ct * 512:(ct + 1) * 512]
                                  yk = ('YB', tt)
                                  if idx == 0:
                                      P.emit('dve', lambda e: e.scalar_tensor_tensor(
                                          out=acc, in0=SGT[:, k, :], scalar=1.0, in1=PS[:, 3584:4096], op0=ALU.add, op1=ALU.mult),
                                          reads=[('SGT', k), ('ps', 7)], writes=[yk])
                                  else:
                                      P.emit('dve', lambda e: e.scalar_tensor_tensor(
                                          out=MT[:], in0=SGT[:, k, :], scalar=1.0, in1=PS[:, 3584:4096], op0=ALU.add, op1=ALU.mult),
                                          reads=[('SGT', k), ('ps', 7)], writes=['MT'])
                                      if idx == 1:
                                          P.emit('dve', lambda e: e.tensor_tensor(out=acc, in0=acc, in1=MT[:], op=ALU.add),
                                                 reads=[yk, 'MT'], writes=[yk])
                                      else:
                                          P.emit('dve', lambda e: e.tensor_tensor(out=MT[:], in0=acc, in1=MT[:], op=ALU.add),
                                                 reads=[yk, 'MT'], writes=['MT'])
                                          P.emit('dve', lambda e: e.tensor_scalar(out=MBv[tt][:, ct * 512:(ct + 1) * 512], in0=MT[:], scalar1=0.5,
                                                                                  scalar2=None, op0=ALU.mult),
                                                 reads=['MT'], writes=mbk[tt])
                              p1.half = 1
                              p2.half = 2
                              out.append(p1)
                              out.append(p2)
                      for tt in range(2):
                          def tr(tt=tt):
                              P.emit('pe', [lambda e, c=c: e.transpose(
                                  PS[:, 3072 + c * 64:3072 + (c + 1) * 64].bitcast(BF16), MBv[tt][:, c * 128:(c + 1) * 128], IDENT[:])
                                  for c in range(8)], reads=mbk[tt] + ['IDENT'], writes=[('ps', 6)])
                              P.emit('dve', lambda e: e.tensor_copy(
                                  out=MG[:, :, tt * 128:(tt + 1) * 128],
                                  in_=PS[:, 3072:3584].bitcast(BF16).rearrange("p (c q) -> p c q", c=8)),
                                  reads=[('ps', 6)], writes=[('MG', c) for c in range(8)])

                          def o1(tt=tt):
                              wo0 = allslots[NO]['wg']
                              P.emit('pe', [lambda e, c=c: e.matmul(
                                  PS[:, 3072:3584], lhsT=MG[:, c, tt * 128:(tt + 1) * 128], rhs=WR[:, wo0, c * 512:(c + 1) * 512],
                                  start=(c == 0), stop=(c == 7)) for c in range(8)],
                                  reads=[('MG', c) for c in range(8)] + [('WR', wo0)], writes=[('ps', 6)])

                          def o2(tt=tt):
                              wo1 = allslots[NO]['wb']
                              tok0 = j * T + tt * 128
                              P.emit('pe', [lambda e, c=c: e.matmul(
                                  PS[:, 3584:4096], lhsT=MG[:, c, tt * 128:(tt + 1) * 128], rhs=WR[:, wo1, c * 512:(c + 1) * 512],
                                  start=(c == 0), stop=(c == 7)) for c in range(8)],
                                  reads=[('MG', c) for c in range(8)] + [('WR', wo1)], writes=[('ps', 7)])
                              P.dma('sp', XTOK[:], xtok[tok0:tok0 + 128, :], [], ['XTOK'], 'xtok')
                              for ct in range(2):
                                  P.emit('dve', lambda e: e.scalar_tensor_tensor(
                                      out=YB[:, tt, ct * 512:(ct + 1) * 512], in0=XTOK[:, ct * 512:(ct + 1) * 512], scalar=float(DN_ALPHA),
                                      in1=PS[:, 3072 + ct * 512:3072 + (ct + 1) * 512], op0=ALU.mult, op1=ALU.add),
                                      reads=['XTOK', ('ps', 6 + ct)], writes=[('YB', tt)])
                          tr.half = 0
                          o1.half = 1
                          o2.half = 2
                          out.append(tr)
                          out.append(o1)
                          out.append(o2)

                      def ln_tail(part):
                          for tt in range(2):
                              tok0 = j * T + tt * 128
                              yk = ('YB', tt)
                              sk = ('ST', tt)
                              if part == 0:
                                  P.emit('dve', lambda e: e.tensor_reduce(out=ST[:, tt, 0:1], in_=YB[:, tt, :], axis=AX.X, op=ALU.add),
                                         reads=[yk], writes=[sk])
                                  P.emit('dve', lambda e: e.tensor_scalar(out=ST[:, tt, 1:2], in0=ST[:, tt, 0:1], scalar1=-1.0 / D, scalar2=None, op0=ALU.mult),
                                         reads=[sk], writes=[sk])
                                  P.emit('dve', lambda e: e.tensor_scalar(out=YB[:, tt, :], in0=YB[:, tt, :], scalar1=ST[:, tt, 1:2], scalar2=None, op0=ALU.add),
                                         reads=[sk, yk], writes=[yk])
                                  P.emit('pool', lambda e: e.tensor_tensor(out=XTOK[:], in0=YB[:, tt, :], in1=YB[:, tt, :], op=ALU.mult),
                                         reads=[yk], writes=['XTOK'])
                                  P.emit('dve', lambda e: e.tensor_reduce(out=ST[:, tt, 2:3], in_=XTOK[:], axis=AX.X, op=ALU.add),
                                         reads=['XTOK'], writes=[sk])
                              else:
                                  P.emit('act', lambda e: e.activation(out=ST[:, tt, 3:4], in_=ST[:, tt, 2:3], func=AF.Sqrt, bias=EPS[:, 1:2], scale=1.0 / D),
                                         reads=[sk, 'EPS'], writes=[sk])
                                  P.emit('dve', lambda e: e.reciprocal(out=ST[:, tt, 4:5], in_=ST[:, tt, 3:4]), reads=[sk], writes=[sk])
                                  P.emit('dve', lambda e: e.scalar_tensor_tensor(out=YB[:, tt, :], in0=YB[:, tt, :], scalar=ST[:, tt, 4:5], in1=LNG[:],
                                                                                 op0=ALU.mult, op1=ALU.mult),
                                         reads=[sk, yk, 'LNG'], writes=[yk])
                                  P.emit('pool', lambda e: e.tensor_tensor(out=YB[:, tt, :], in0=YB[:, tt, :], in1=LNB[:], op=ALU.add),
                                         reads=[yk, 'LNB'], writes=[yk])
                                  P.dma('pool', ydr[tok0:tok0 + 128, :], YB[:, tt, :], [yk], [('ydr', n, j, tt)], ('yst', tt))
                      out.append(lambda: ln_tail(0))
                      out.append(lambda: None)
                      out.append(lambda: ln_tail(1))
                      return out
                  jobs.extend(build_jobs())
              ng_flush()
              while jobs:
                  pop_job()

        except _StopEmission:
            pass
        P.final_wait('pool')

        sems = {}
        for k in P.cnt:
            sems[k] = es.enter_context(nc.semaphore("s%d" % len(sems)))
        block = es.enter_context(nc.Block())

        def run(name, e):
            for waits, fns, key, inc in P.engs[name]:
                for (k, v) in waits:
                    e.wait_ge(sems[k], v)
                if fns is None:
                    continue
                for f in fns[:-1]:
                    f(e)
                fns[-1](e).then_inc(sems[key], inc)

        @block.tensor
        def _(e):
            run('pe', e)

        @block.scalar
        def _(e):
            run('act', e)

        @block.vector
        def _(e):
            run('dve', e)

        @block.gpsimd
        def _(e):
            run('pool', e)

        @block.sync
        def _(e):
            run('sp', e)
    return nc, P


def _tile_k8(W):
    C = W.shape[1]
    return np.ascontiguousarray(W.reshape(8, 128, C).transpose(1, 0, 2).reshape(128, 8 * C))


def _weight_tiles(w_in, w_mem_kv, w_branch, w_out):
    wt = np.zeros((NT, 128, 4096), np.float32)
    qa_cols = np.concatenate([np.arange(h * 64, (h + 1) * 64) for h in (0, 4, 1, 5, 2, 6, 3, 7)])
    wt[0] = _tile_k8(w_in[:, 1792:2304])
    wt[1] = _tile_k8(w_in[:, 2304:2816])
    wt[2] = _tile_k8(w_in[:, 0:512][:, qa_cols])
    wt[3] = _tile_k8(w_in[:, 768:1280])
    wt[4] = _tile_k8(w_in[:, 1280:1792])
    wt[5] = _tile_k8(w_in[:, 2816:3328])
    wt[6] = _tile_k8(w_in[:, 3328:3840])
    wt[7] = _tile_k8(w_in[:, 3840:4352])
    for ct in range(2):
        for nb in range(3):
            base = 8 + ct * 6 + nb * 2
            wt[base] = _tile_k8(w_in[:, 4352 + nb * 1024 + ct * 512:4352 + nb * 1024 + (ct + 1) * 512])
            wb = w_branch[nb][:, ct * 512:(ct + 1) * 512]
            if nb < 2:
                wt[base + 1, 0:64, :] = wb.reshape(8, 64, 512).transpose(1, 0, 2).reshape(64, 4096)
            else:
                wt[base + 1, :, 0:2048] = wb.reshape(4, 128, 512).transpose(1, 0, 2).reshape(128, 2048)
    wt[24] = _tile_k8(w_out[:, 0:512])
    wt[25] = _tile_k8(w_out[:, 512:1024])
    wt[26, :, 0:2048] = _tile_k8(w_in[:, 512:768])
    wt[27] = _tile_k8(w_mem_kv[:, 0:512])
    wt[28] = _tile_k8(w_mem_kv[:, 512:1024])
    return wt


def _rope_table(pos):
    pos2 = np.stack([pos // GRID_W, pos % GRID_W], axis=-1).astype(np.float32)
    inv_freq = (np.float32(10000.0) ** (-np.arange(16, dtype=np.float32) / np.float32(16))).astype(np.float32)
    ang = (pos2[:, :, None] * inv_freq).astype(np.float32)
    return np.ascontiguousarray(np.concatenate([np.cos(ang).reshape(-1, 32), np.sin(ang).reshape(-1, 32)], axis=1).astype(np.float32))


def _rv_table(S, q0, nq):
    R = S // GRID_W
    nl = nq // 128
    rv = np.zeros((128, 5, 7, 2), np.float32)
    kr_l = np.arange(128) // 64
    for i in range(nl):
        cls = blk_class(i, nl)
        gi = q0 // 128 + i
        for o in OFFS[cls]:
            for a in range(2):
                qr = 2 * gi + a
                q_wr = min(max(qr - 4, 0), R - 8)
                kr = 2 * (gi + o) + kr_l
                ok = (kr >= 0) & (kr < R) & (kr >= q_wr) & (kr < q_wr + 8)
                rv[:, cls, o + 3, a] = ok.astype(np.float32)
    return np.ascontiguousarray(rv.reshape(128, 70))


def _bias_tables(rpb):
    k = np.arange(128)
    q = np.arange(128)
    krl, kc = k // 64, k % 64
    qrl, qc = q // 64, q % 64
    bt = np.zeros((128, 7, 8, 128), np.float32)
    cm = np.zeros((128, 7, 128), np.float32)
    q_wc = np.clip(qc - 8, 0, GRID_W - 16)
    dc = kc[:, None] - qc[None, :] + 15
    col_ok = (kc[:, None] >= q_wc[None, :]) & (kc[:, None] < q_wc[None, :] + 16)
    for oi in range(7):
        o = oi - 3
        drr = 2 * o + krl[:, None] - qrl[None, :] + 7
        ok = col_ok & (drr >= 0) & (drr <= 14) & (dc >= 0) & (dc <= 30)
        cm[:, oi, :] = ok
        bt[:, oi, :, :] = rpb[[0, 2, 4, 6, 1, 3, 5, 7]][:, np.clip(drr, 0, 14), np.clip(dc, 0, 30)].transpose(1, 0, 2)
    return np.ascontiguousarray(bt.reshape(128, 7 * 8 * 128)), np.ascontiguousarray(cm.reshape(128, 7 * 128))


def _xt_tiles(xT):
    ns = xT.shape[1] // T
    return np.ascontiguousarray(xT.reshape(8, 128, ns, T).transpose(2, 1, 0, 3).reshape(ns, 128, 8 * T))


_prog_cache = {}


def _make_in_maps(x_prompt, x_sample, mem_prompt, mem_sample, w_in, q_norm, k_norm, rpb, w_mem_kv,
                  w_branch, w_out, ln_g, ln_b):
    x_prompt = np.asarray(x_prompt, np.float32)
    x_sample = np.asarray(x_sample, np.float32)
    mem_prompt = np.asarray(mem_prompt, np.float32)
    mem_sample = np.asarray(mem_sample, np.float32)
    Bp, Sp, _ = x_prompt.shape
    Bs, Ss, _ = x_sample.shape
    assert Bp == 2 and Bs == 8
    Pq = Sp // 4
    wt = _weight_tiles(np.asarray(w_in[0], np.float32), np.asarray(w_mem_kv[0], np.float32),
                       np.asarray(w_branch[0], np.float32), np.asarray(w_out[0], np.float32))
    biastab, cmask = _bias_tables(np.asarray(rpb[0], np.float32))
    qn = np.ascontiguousarray(np.tile(np.asarray(q_norm[0], np.float32)[None, :], (128, 1)))
    kn = np.ascontiguousarray(np.tile(np.asarray(k_norm[0], np.float32)[None, :], (128, 1)))
    lng = np.ascontiguousarray(np.tile(np.asarray(ln_g[0], np.float32)[None, :], (128, 1)))
    lnb = np.ascontiguousarray(np.tile(np.asarray(ln_b[0], np.float32)[None, :], (128, 1)))
    rope_s = _rope_table(np.arange(Ss))
    rope_p = _rope_table(np.arange(Sp))
    rv_s = _rv_table(Ss, 0, Ss)
    xkv_p = [np.ascontiguousarray(x_prompt[b].T) for b in range(Bp)]
    xkv_t = [_xt_tiles(x) for x in xkv_p]
    memT_p = [np.ascontiguousarray(mem_prompt[b].T) for b in range(Bp)]

    in_maps = []
    for c in range(8):
        m = dict(wt=wt, biastab=biastab, cmask=cmask, qn=qn, kn=kn, lng=lng, lnb=lnb)
        xs = x_sample[c]
        xq = np.zeros((D, Ss + 2 * T), np.float32)
        xq[:, T:T + Ss] = xs.T
        m['s_xq'] = _xt_tiles(xq)
        m['s_xtok'] = np.ascontiguousarray(xs)
        m['s_ropekv'] = rope_s
        m['s_ropeq'] = rope_s
        m['s_rv'] = rv_s
        m['s_memT'] = np.ascontiguousarray(mem_sample[c].T)
        b, qd = c // 4, c % 4
        q0 = qd * Pq
        xq = np.zeros((D, Pq + 2 * T), np.float32)
        lo, hi = max(q0 - T, 0), min(q0 + Pq + T, Sp)
        xq[:, lo - (q0 - T):hi - (q0 - T)] = xkv_p[b][:, lo:hi]
        m['p_xq'] = _xt_tiles(xq)
        m['p_xkv'] = xkv_t[b]
        m['p_xtok'] = np.ascontiguousarray(x_prompt[b, q0:q0 + Pq])
        m['p_ropekv'] = rope_p
        m['p_ropeq'] = np.ascontiguousarray(rope_p[q0:q0 + Pq])
        m['p_rv'] = _rv_table(Sp, q0, Pq)
        m['p_memT'] = memT_p[b]
        in_maps.append(m)
    return in_maps


def kernel(x_prompt, x_sample, mem_prompt, mem_sample, w_in, q_norm, k_norm, rpb, w_mem_kv,
           w_branch, w_out, ln_g, ln_b):
    Bp, Sp, _ = x_prompt.shape
    Bs, Ss, _ = x_sample.shape
    Pq = Sp // 4
    in_maps = _make_in_maps(x_prompt, x_sample, mem_prompt, mem_sample, w_in, q_norm, k_norm, rpb, w_mem_kv,
                            w_branch, w_out, ln_g, ln_b)
    key = (Ss, Sp)
    if key not in _prog_cache:
        _prog_cache[key] = build_program(Ss, Sp)[0]
    nc = _prog_cache[key]
    res = run_bass_kernel_spmd(nc, in_maps, core_ids=list(range(8)))
    y_prompt = np.zeros((Bp, Sp, D), np.float32)
    y_sample = np.zeros((Bs, Ss, D), np.float32)
    for c in range(8):
        r = res.results[c]
        y_sample[c] = r['s_y']
        b, qd = c // 4, c % 4
        y_prompt[b, qd * Pq:(qd + 1) * Pq] = r['p_y']
    return (y_prompt, y_sample)
```

```python
import types
import collections
import numpy as np
from contextlib import ExitStack
import concourse.bass as bass
import concourse.mybir as mybir
from concourse.bass_utils import run_bass_kernel_spmd

F32 = mybir.dt.float32
BF16 = mybir.dt.bfloat16
ALU = mybir.AluOpType
AF = mybir.ActivationFunctionType
AX = mybir.AxisListType

D = 1024
T = 256
NSLOT = 2
NPB = 3
NPAR = 4
NT = 29
TILE_EL = [4096] * 8 + [4096, 4096, 4096, 4096, 4096, 2048] * 2 + [16] * 4 + [4096, 4096] + [2048, 4096, 4096]
GRID_W = 64
DN_ALPHA = 2.0 ** 0.25
DN_BETA = 8.0 ** -0.25
EPS_QK = 1e-6
EPS_LN = 1e-5
OFFS = {0: [-2, -1, 0, 1, 2, 3], 1: [-2, -1, 0, 1, 2], 2: [-2, -1, 0, 1, 2], 3: [-2, -1, 0, 1, 2],
        4: [-3, -2, -1, 0, 1, 2]}


def blk_class(i, nl):
    if i == 0:
        return 0
    if i == 1:
        return 1
    if i == nl - 1:
        return 4
    if i == nl - 2:
        return 3
    return 2


def _freeze(fn):
    if fn.__closure__ is None:
        return fn
    cells = []
    for c in fn.__closure__:
        try:
            cells.append(types.CellType(c.cell_contents))
        except ValueError:
            cells.append(c)
    g = types.FunctionType(fn.__code__, fn.__globals__, fn.__name__, fn.__defaults__, tuple(cells))
    g.__kwdefaults__ = fn.__kwdefaults__
    return g


class _StopEmission(Exception):
    pass


DEBUG_STOP = [None]


def _phase(name):
    if DEBUG_STOP[0] is not None and name == DEBUG_STOP[0]:
        raise _StopEmission()


class Prog:
    def __init__(self):
        self.engs = {'pe': [], 'act': [], 'dve': [], 'pool': [], 'sp': []}
        self.cnt = {}
        self.waited = {e: {} for e in self.engs}
        self.lw = {}
        self.lr = {}
        self.const = set()
        self.n_inst = 0

    def _deps(self, reads, writes):
        d = {}

        def add(t):
            if t is None:
                return
            k, v = t
            if d.get(k, 0) < v:
                d[k] = v
        for k in reads:
            add(self.lw.get(k))
        for k in writes:
            add(self.lw.get(k))
            for kk, vv in self.lr.get(k, {}).items():
                add((kk, vv))
        return d

    def _filter(self, eng, d):
        out = []
        w = self.waited[eng]
        for k, v in d.items():
            if w.get(k, 0) < v:
                w[k] = v
                out.append((k, v))
        return out

    def _commit(self, ticket, reads, writes):
        k0, v0 = ticket
        for k in reads:
            if k in self.const:
                continue
            r = self.lr.setdefault(k, {})
            if r.get(k0, 0) < v0:
                r[k0] = v0
        for k in writes:
            self.lw[k] = ticket
            self.lr[k] = {}

    def emit(self, eng, fns, reads=(), writes=()):
        if callable(fns):
            fns = [fns]
        fns = [_freeze(f) for f in fns]
        d = self._deps(reads, writes)
        if eng == 'pe':
            d.pop(('eng', 'pe'), None)
        waits = self._filter(eng, d)
        key = ('eng', eng)
        self.cnt[key] = self.cnt.get(key, 0) + 1
        ticket = (key, self.cnt[key])
        self.engs[eng].append((waits, fns, key, 1))
        self.waited[eng][key] = max(self.waited[eng].get(key, 0), 0)
        self._commit(ticket, reads, writes)
        self.n_inst += len(fns)
        return ticket

    def dma(self, eng, out, in_, reads, writes, semkey):
        waits = self._filter(eng, self._deps(reads, writes))
        key = ('dma', semkey)
        self.cnt[key] = self.cnt.get(key, 0) + 16
        ticket = (key, self.cnt[key])
        self.engs[eng].append((waits, [lambda e, o=out, i=in_: e.dma_start(out=o, in_=i)], key, 16))
        self._commit(ticket, reads, writes)
        self.n_inst += 1
        return ticket

    def final_wait(self, eng):
        waits = self._filter(eng, dict(self.cnt))
        self.engs[eng].append((waits, None, None, 0))


def build_program(Ss, Sp):
    Pq = Sp // 4
    segs = [dict(n='s', Skv=Ss, nq=Ss, kv_in_q=True), dict(n='p', Skv=Sp, nq=Pq, kv_in_q=False)]
    SKV_MAX = max(Ss, Sp)
    nc = bass.Bass("TRN2", target_bir_lowering=False)
    P = Prog()

    def din(name, shape, dt=F32):
        return nc.dram_tensor(name, list(shape), dt, kind="ExternalInput").ap()

    dr = {}
    for sg in segs:
        n = sg['n']
        dr[n + '_xq'] = din(n + '_xq', [(sg['nq'] + 2 * T) // T, 128, 8 * T])
        if not sg['kv_in_q']:
            dr[n + '_xkv'] = din(n + '_xkv', [sg['Skv'] // T, 128, 8 * T])
        dr[n + '_xtok'] = din(n + '_xtok', [sg['nq'], D])
        dr[n + '_ropekv'] = din(n + '_ropekv', [sg['Skv'], 64])
        dr[n + '_ropeq'] = din(n + '_ropeq', [sg['nq'], 64])
        dr[n + '_rv'] = din(n + '_rv', [128, 70])
        dr[n + '_memT'] = din(n + '_memT', [D, 256])
        dr[n + '_y'] = nc.dram_tensor(n + '_y', [sg['nq'], D], F32, kind="ExternalOutput").ap()
    wt = din('wt', [NT, 128, 4096])
    biastab = din('biastab', [128, 7 * 8 * 128])
    cmask_d = din('cmask', [128, 7 * 128])
    qn_d = din('qn', [128, 64])
    kn_d = din('kn', [128, 64])
    lng_d = din('lng', [128, D])
    lnb_d = din('lnb', [128, D])
    wscr = nc.dram_tensor('wscr', [NT, 128, 4096], BF16, kind="Internal").ap()

    es = ExitStack()
    with es:
        sb_total = [0]

        def sb(name, shape, dt):
            nb = int(np.prod(shape[1:])) * (2 if dt == BF16 else 4)
            sb_total[0] += nb
            return es.enter_context(nc.sbuf_tensor(name, list(shape), dt))

        KA = sb('KA', [128, SKV_MAX], BF16)
        VA = sb('VA', [128, SKV_MAX // 128, 2, 65], BF16)
        XT = sb('XT', [128, 3, 8, T], BF16)
        WR = sb('WR', [128, NSLOT, 4096], BF16)
        QA = sb('QA', [128, 2, 4, 128], BF16)
        SZA2 = [sb('SZA%d' % k, [64, 8, T], BF16) for k in range(2)]
        QB = sb('QB', [128, 4, T], BF16)
        SZB2 = [sb('SZB%d' % k, [64, 8, T], BF16) for k in range(2)]
        QM = sb('QM', [128, 4, T], BF16)
        SZM2 = [sb('SZM%d' % k, [128, 4, T], BF16) for k in range(2)]
        KB = sb('KB', [128, 4, 3, T], BF16)
        VB = sb('VB', [128, 3, 2, 8, 65], BF16)
        KM = sb('KM', [128, 4, 256], BF16)
        VM = sb('VM', [128, 2, 512], BF16)
        MG = sb('MG', [128, 8, T], BF16)
        PB = sb('PB', [128, NPB, 1024], BF16)
        ET = sb('ET', [128, 7, 8, 128], BF16)
        RV = sb('RV', [128, 2, 70], F32)
        XTOK = sb('XTOK', [128, D], F32)
        YB = sb('YB', [128, 2, D], F32)
        OSB = sb('OSB', [128, 1024], F32)
        RZ = sb('RZ', [128, 1024], F32)
        LNG = sb('LNG', [128, D], F32)
        LNB = sb('LNB', [128, D], F32)
        QN = sb('QN', [128, 64], F32)
        KN = sb('KN', [128, 64], F32)
        RT = sb('RT', [128, 2, 2, 64], F32)
        GT = sb('GT', [128, NPAR, 4, 32], F32)
        XS = sb('XS', [128, 512], F32)
        TMP = sb('TMP', [128, 2, 256], F32)
        ORO = sb('ORO', [128, 512], F32)
        QR = sb('QR', [128, 512], BF16)
        SS = sb('SS', [128, 24], F32)
        ST = sb('ST', [128, 2, 8], F32)
        SGT = sb('SGT', [128, 2, 512], BF16)
        MT = sb('MT', [128, 512], F32)
        IDENTF = sb('IDENTF', [128, 128], F32)
        IDENT = sb('IDENT', [128, 128], BF16)
        ONESF = sb('ONESF', [128, 64], F32)
        ONESB = sb('ONESB', [128, 128], BF16)
        EPS = sb('EPS', [128, 2], F32)
        PS = es.enter_context(nc.psum_tensor('PS', [128, 4096], F32))
        assert sb_total[0] <= 210000, sb_total[0]
        WKVA = MG

        P.const |= {'IDENT', 'ONESF', 'ONESB', 'EPS', 'VA1', 'VB1', 'QN', 'KN', 'LNG', 'LNB', 'ET'}
        xbank_ctr = [0]

        def next_bank():
            b = [6, 7, 4, 5, 0, 1, 2, 3][xbank_ctr[0] % 8]
            xbank_ctr[0] += 1
            return b

        P.emit('pool', lambda e: e.memset(VA[:, :, :, 64:65], 1.0), writes=['VA1'])
        P.emit('pool', lambda e: e.memset(VB[:, :, :, :, 64:65], 1.0), writes=['VB1'])
        P.emit('pool', lambda e: e.memset(ONESF[:], 1.0), writes=['ONESF'])
        P.emit('pool', lambda e: e.memset(ONESB[:], 1.0), writes=['ONESB'])
        P.emit('pool', lambda e: e.memset(EPS[:, 0:1], EPS_QK), writes=['EPS0'])
        P.emit('pool', lambda e: e.memset(EPS[:, 1:2], EPS_LN), reads=['EPS0'], writes=['EPS'])
        P.emit('pool', lambda e: e.memset(IDENTF[:], 0.0), writes=['IDENTF'])
        P.emit('pool', lambda e: e.affine_select(out=IDENTF[:], in_=IDENTF[:], pattern=[[-1, 128]],
                                                 compare_op=ALU.not_equal, fill=1.0, base=0,
                                                 channel_multiplier=1), reads=['IDENTF'], writes=['IDENTF'])
        P.emit('pool', lambda e: e.tensor_copy(out=IDENT[:], in_=IDENTF[:]), reads=['IDENTF'], writes=['IDENT'])
        P.dma('sp', QN[:], qn_d[:], [], ['QN'], 'c0')
        P.dma('sp', KN[:], kn_d[:], [], ['KN'], 'c1')
        P.dma('sp', LNG[:], lng_d[:], [], ['LNG'], 'c2')
        P.dma('sp', LNB[:], lnb_d[:], [], ['LNB'], 'c3')
        for si, sg in enumerate(segs):
            P.dma('sp', RV[:, si, :], dr[sg['n'] + '_rv'][:], [], [('RV', si)], ('c4', si))
            P.const.add(('RV', si))
        P.dma('sp', RZ[:, 0:896], cmask_d[:], [], ['RZ'], 'c6')
        for oi in range(7):
            P.dma('sp', OSB[:], biastab[:, oi * 1024:(oi + 1) * 1024], [], ['OSB'], 'c7')
            P.emit('act', lambda e: e.activation(out=OSB[:], in_=OSB[:], func=AF.Exp), reads=['OSB'], writes=['OSB'])
            P.emit('dve', lambda e: e.tensor_tensor(
                out=ET[:, oi, :, :], in0=OSB[:].rearrange("p (h q) -> p h q", h=8),
                in1=RZ[:, oi * 128:(oi + 1) * 128].unsqueeze(1).broadcast_to([128, 8, 128]), op=ALU.mult),
                reads=['OSB', 'RZ'], writes=[('ETw', oi)])
        P.emit('dve', lambda e: e.tensor_copy(out=SS[:, 0:1], in_=EPS[:, 0:1]),
               reads=[('ETw', oi) for oi in range(7)] + ['EPS'], writes=['ET'] + [('SS', q) for q in range(NPAR)])

        wr_g = [0]

        def wload(ti):
            slot = wr_g[0] % NSLOT
            wr_g[0] += 1
            nel = TILE_EL[ti]
            P.dma('sp', WR[:, slot, 0:nel], wscr[ti, :, 0:nel], [('wscr', ti)], [('WR', slot)], ('wr', slot))
            return slot

        conv_next = [0]
        conv_order = [27, 28] + list(range(0, 20)) + [24, 25]

        def convert_one():
            if conv_next[0] >= len(conv_order):
                return
            ti = conv_order[conv_next[0]]
            conv_next[0] += 1
            slot = wr_g[0] % NSLOT
            wr_g[0] += 1
            nel = TILE_EL[ti]
            P.dma('pool', WR[:, slot, 0:nel], wt[ti, :, 0:nel], [], [('WR', slot)], ('wrc', slot))
            P.dma('sp', wscr[ti, :, 0:nel], WR[:, slot, 0:nel], [('WR', slot)], [('wscr', ti)], ('ws', slot))

        xt_g = [0]

        def xt_load(src, col0):
            slot = xt_g[0] % 3
            xt_g[0] += 1
            P.dma('pool', XT[:, slot].rearrange("p c t -> p (c t)"), src[col0 // T], [], [('XT', slot)], ('xt', slot))
            return slot

        def rope_chain(H, gains_key, gains, rt_key, rt_ap, dst_keys, dst_ap, par=0, full=True):
            W = H * 32
            xo = 0 if full else par * 128
            to = 0 if full else par * 64
            sb0 = 0 if full else par * 2

            def K(name, *extra):
                if full:
                    return [(name,) + extra + (q,) for q in range(NPAR)]
                return [(name,) + extra + (par,)]
            xs_ap = XS[:, xo:xo + H * 64]
            cos = rt_ap[:, 0:32].rearrange("p (a f) -> p a f", a=2)
            sin = rt_ap[:, 32:64].rearrange("p (a f) -> p a f", a=2)
            g4 = gains[:].rearrange("p (a j f) -> p a j f", a=2, j=2)
            g1 = g4[:, :, 0, :]
            g2 = g4[:, :, 1, :]
            gp = 0 if full else par
            gt = [GT[:, gp, k, :].rearrange("p (a f) -> p a f", a=2) for k in range(4)]
            gk = [[('GT', k, gp)] for k in range(4)]
            P.emit('pool', lambda e: e.tensor_tensor(out=gt[0], in0=cos, in1=g1, op=ALU.mult), reads=[rt_key, gains_key], writes=gk[0])
            P.emit('pool', lambda e: e.tensor_tensor(out=gt[1], in0=sin, in1=g2, op=ALU.mult), reads=[rt_key, gains_key], writes=gk[1])
            P.emit('pool', lambda e: e.tensor_tensor(out=gt[2], in0=cos, in1=g2, op=ALU.mult), reads=[rt_key, gains_key], writes=gk[2])
            P.emit('pool', lambda e: e.tensor_tensor(out=gt[3], in0=sin, in1=g1, op=ALU.mult), reads=[rt_key, gains_key], writes=gk[3])
            x5 = xs_ap.rearrange("p (h a j f) -> p h a j f", h=H, a=2, j=2)
            x1 = x5[:, :, :, 0, :]
            x2 = x5[:, :, :, 1, :]
            oro = ORO[:, xo:xo + H * 64]
            o5 = oro.rearrange("p (h a j f) -> p h a j f", h=H, a=2, j=2)
            tm = [TMP[:, k, to:to + W].rearrange("p (h a f) -> p h a f", h=H, a=2) for k in range(2)]
            bcg = [g.unsqueeze(1).broadcast_to([128, H, 2, 16]) for g in gt]
            ssv = SS[:, sb0:sb0 + H]
            sdv = SS[:, 8 + sb0:8 + sb0 + H]
            rsv = SS[:, 16 + sb0:16 + sb0 + H]
            P.emit('dve', lambda e: e.tensor_tensor(out=oro, in0=xs_ap, in1=xs_ap, op=ALU.mult), reads=K('XS'), writes=K('ORO'))
            P.emit('dve', lambda e: e.tensor_reduce(out=ssv, in_=oro.rearrange("p (h d) -> p h d", h=H), axis=AX.X, op=ALU.add),
                   reads=K('ORO'), writes=K('SS'))
            P.emit('act', lambda e: e.activation(out=sdv, in_=ssv, func=AF.Sqrt, bias=EPS[:, 0:1], scale=1.0 / 64),
                   reads=K('SS') + ['EPS'], writes=K('SS'))
            P.emit('dve', lambda e: e.tensor_tensor(out=tm[0], in0=x1, in1=bcg[0], op=ALU.mult), reads=K('XS') + gk[0], writes=K('TMP', 0))
            P.emit('dve', lambda e: e.tensor_tensor(out=tm[1], in0=x2, in1=bcg[1], op=ALU.mult), reads=K('XS') + gk[1], writes=K('TMP', 1))
            P.emit('dve', lambda e: e.tensor_tensor(out=o5[:, :, :, 0, :], in0=tm[0], in1=tm[1], op=ALU.subtract),
                   reads=K('TMP', 0) + K('TMP', 1), writes=K('ORO'))
            P.emit('dve', lambda e: e.tensor_tensor(out=tm[0], in0=x2, in1=bcg[2], op=ALU.mult), reads=K('XS') + gk[2], writes=K('TMP', 0))
            P.emit('dve', lambda e: e.tensor_tensor(out=tm[1], in0=x1, in1=bcg[3], op=ALU.mult), reads=K('XS') + gk[3], writes=K('TMP', 1))
            P.emit('dve', lambda e: e.tensor_tensor(out=o5[:, :, :, 1, :], in0=tm[0], in1=tm[1], op=ALU.add),
                   reads=K('TMP', 0) + K('TMP', 1), writes=K('ORO'))
            P.emit('dve', lambda e: e.reciprocal(out=rsv, in_=sdv), reads=K('SS'), writes=K('SS'))
            P.emit('dve', lambda e: e.tensor_tensor(
                out=dst_ap.rearrange("p (h d) -> p h d", h=H), in0=oro.rearrange("p (h d) -> p h d", h=H),
                in1=rsv.unsqueeze(2).broadcast_to([128, H, 64]), op=ALU.mult),
                reads=K('ORO') + K('SS'), writes=dst_keys)

        pending_ng = [None]

        def ng_flush():
            if pending_ng[0] is not None:
                f = pending_ng[0]
                pending_ng[0] = None
                f()

        def norm_gate(sz_view, sz_keys):
            ng_flush()
            P.emit('dve', lambda e: e.tensor_copy(out=OSB[0:65, :], in_=PS[0:65, 2048:3072]),
                   reads=[('ps', 4), ('ps', 5)], writes=['OSB'])
            P.emit('dve', lambda e: e.reciprocal(out=OSB[64:65, :], in_=OSB[64:65, :]), reads=['OSB'], writes=['OSB'])

            def part1():
                P.emit('pe', [lambda e: e.matmul(PS[0:64, 3072:3584], lhsT=ONESF[64:65, 0:64], rhs=OSB[64:65, 0:512], start=True, stop=True),
                              lambda e: e.matmul(PS[0:64, 3584:4096], lhsT=ONESF[64:65, 0:64], rhs=OSB[64:65, 512:1024], start=True, stop=True)],
                       reads=['OSB', 'ONESF'], writes=[('ps', 6), ('ps', 7)])
                P.emit('dve', lambda e: e.tensor_tensor(out=RZ[0:64, :].rearrange("p (h q) -> p h q", h=8),
                                                        in0=PS[0:64, 3072:4096].rearrange("p (h q) -> p h q", h=8),
                                                        in1=sz_view, op=ALU.mult),
                       reads=[('ps', 6), ('ps', 7)] + sz_keys, writes=['RZ'])
                P.emit('dve', lambda e: e.tensor_tensor(out=sz_view, in0=OSB[0:64, :].rearrange("p (h q) -> p h q", h=8),
                                                        in1=RZ[0:64, :].rearrange("p (h q) -> p h q", h=8), op=ALU.mult),
                       reads=['OSB', 'RZ'], writes=sz_keys)
            pending_ng[0] = part1

        pb_g = [0]

        def next_pb():
            s = pb_g[0] % NPB
            pb_g[0] += 1
            return s

        jobs = collections.deque()
        mid_job = [False]

        def pop_job():
            f = jobs.popleft()
            f()
            mid_job[0] = getattr(f, 'half', 0) == 1
        try:
          for si, sg in enumerate(segs):
              n = sg['n']
              Skv, nq = sg['Skv'], sg['nq']
              nsteps = nq // T
              nl = nq // 128
              nkt = Skv // 128
              xq = dr[n + '_xq']
              if sg['kv_in_q']:
                  kvsrc, kvoff = xq, T
              else:
                  kvsrc, kvoff = dr[n + '_xkv'], 0
              ropekv = dr[n + '_ropekv'].rearrange("(c p) f -> p c f", p=128)
              ropeq = dr[n + '_ropeq'].rearrange("(c p) f -> p c f", p=128)
              xtok = dr[n + '_xtok']
              ydr = dr[n + '_y']

              _phase('phase1')
              P.dma('pool', WKVA[:].rearrange("p c n -> p (c n)"), wt[26, :, 0:2048], [], [('MG', c) for c in range(8)], 'c5')
              mgk = [('MG', c) for c in range(8)]
              for ci in range(Skv // T):
                  if si == 0:
                      convert_one()
                  xs_slot = xt_load(kvsrc, kvoff + ci * T)
                  rb = ci % 2
                  P.dma('sp', RT[:, rb], ropekv[:, ci * 2:ci * 2 + 2, :], [], [('RT', rb)], ('rt', rb))
                  for tt in range(2):
                      kt = ci * 2 + tt
                      b = next_bank()
                      P.emit('pe', [lambda e, c=c: e.matmul(
                          PS[:, b * 512:b * 512 + 256], lhsT=XT[:, xs_slot, c, tt * 128:(tt + 1) * 128], rhs=WKVA[:, c, :],
                          start=(c == 0), stop=(c == 7)) for c in range(8)],
                          reads=[('XT', xs_slot)] + mgk, writes=[('ps', b)])
                      kp = kt % NPAR
                      P.emit('act', lambda e: e.activation(out=XS[:, kp * 128:kp * 128 + 128], in_=PS[:, b * 512:b * 512 + 128], func=AF.Copy),
                             reads=[('ps', b)], writes=[('XS', kp)])
                      P.emit('act', lambda e: e.activation(
                          out=VA[:, kt, :, 0:64], in_=PS[:, b * 512 + 128:b * 512 + 256].rearrange("p (k d) -> p k d", k=2), func=AF.Copy),
                          reads=[('ps', b)], writes=[('VA', kt)])
                      rope_chain(2, 'KN', KN, ('RT', rb), RT[:, rb, tt, :], [('QR', kp)], QR[:, kp * 128:kp * 128 + 128], par=kp, full=False)
                      b2 = next_bank()
                      P.emit('pe', lambda e: e.transpose(
                          PS[:, b2 * 512:b2 * 512 + 64].bitcast(BF16), QR[:, kp * 128:kp * 128 + 128], IDENT[:]),
                          reads=[('QR', kp), 'IDENT'], writes=[('ps', b2)])
                      P.emit('act', lambda e: e.activation(
                          out=KA[:, kt * 128:(kt + 1) * 128], in_=PS[:, b2 * 512:b2 * 512 + 64].bitcast(BF16), func=AF.Copy),
                          reads=[('ps', b2)], writes=[('KA', kt)])
              if si == 0:
                  while conv_next[0] < len(conv_order):
                      convert_one()

              _phase('memkv')
              ms = xt_g[0] % 3
              xt_g[0] += 1
              P.dma('pool', XT[:, ms], dr[n + '_memT'].rearrange("(c p) t -> p c t", p=128), [], [('XT', ms)], ('xt', ms))
              ws = wload(27)
              for h in range(4):
                  b = next_bank()
                  P.emit('pe', [lambda e, c=c: e.matmul(
                      PS[:, b * 512:b * 512 + 256], lhsT=WR[:, ws, c * 512 + h * 128:c * 512 + (h + 1) * 128], rhs=XT[:, ms, c, :],
                      start=(c == 0), stop=(c == 7)) for c in range(8)],
                      reads=[('XT', ms), ('WR', ws)], writes=[('ps', b)])
                  P.emit('dve', lambda e: e.tensor_copy(out=KM[:, h, :], in_=PS[:, b * 512:b * 512 + 256]),
                         reads=[('ps', b)], writes=[('KM', h)])
              ws = wload(28)
              for mt in range(2):
                  b = next_bank()
                  P.emit('pe', [lambda e, c=c: e.matmul(
                      PS[:, b * 512:(b + 1) * 512], lhsT=XT[:, ms, c, mt * 128:(mt + 1) * 128], rhs=WR[:, ws, c * 512:(c + 1) * 512],
                      start=(c == 0), stop=(c == 7)) for c in range(8)],
                      reads=[('XT', ms), ('WR', ws)], writes=[('ps', b)])
                  P.emit('dve', lambda e: e.tensor_copy(out=VM[:, mt, :], in_=PS[:, b * 512:(b + 1) * 512]),
                         reads=[('ps', b)], writes=[('VM', mt)])

              _phase('kvb')
              def kvb(jj, xs_):
                  rs = (jj + 1) % 3
                  ws = wload(0)
                  for p in range(4):
                      b = next_bank()
                      P.emit('pe', [lambda e, c=c: e.matmul(
                          PS[:, b * 512:b * 512 + T], lhsT=WR[:, ws, c * 512 + p * 128:c * 512 + (p + 1) * 128], rhs=XT[:, xs_, c, :],
                          start=(c == 0), stop=(c == 7)) for c in range(8)],
                          reads=[('XT', xs_), ('WR', ws)], writes=[('ps', b)])
                      P.emit('dve', lambda e: e.tensor_copy(out=KB[:, p, rs, :], in_=PS[:, b * 512:b * 512 + T]),
                             reads=[('ps', b)], writes=[('KB', rs, p)])
                  ws = wload(1)
                  for tt in range(2):
                      b = next_bank()
                      P.emit('pe', [lambda e, c=c: e.matmul(
                          PS[:, b * 512:(b + 1) * 512], lhsT=XT[:, xs_, c, tt * 128:(tt + 1) * 128], rhs=WR[:, ws, c * 512:(c + 1) * 512],
                          start=(c == 0), stop=(c == 7)) for c in range(8)],
                          reads=[('XT', xs_), ('WR', ws)], writes=[('ps', b)])
                      P.emit('dve', lambda e: e.tensor_copy(
                          out=VB[:, rs, tt, :, 0:64], in_=PS[:, b * 512:(b + 1) * 512].rearrange("p (h d) -> p h d", h=8)),
                          reads=[('ps', b)], writes=[('VB', rs, tt)])

              def fm_proj(ti, ngroups, M, xs_, evac):
                  ws = wload(ti)
                  for g in range(ngroups):
                      b = next_bank()
                      P.emit('pe', [lambda e, c=c: e.matmul(
                          PS[0:M, b * 512:b * 512 + T], lhsT=WR[:, ws, c * 512 + g * M:c * 512 + (g + 1) * M], rhs=XT[:, xs_, c, :],
                          start=(c == 0), stop=(c == 7)) for c in range(8)],
                          reads=[('XT', xs_), ('WR', ws)], writes=[('ps', b)])
                      evac(g, b)

              def fm_proj_zpair(ti, xs_, dst, keyname):
                ws = wload(ti)
                for p in range(4):
                    b = next_bank()
                    P.emit('pe', [lambda e, c=c: e.matmul(
                        PS[:, b * 512:b * 512 + T], lhsT=WR[:, ws, c * 512 + p * 128:c * 512 + (p + 1) * 128], rhs=XT[:, xs_, c, :],
                        start=(c == 0), stop=(c == 7)) for c in range(8)],
                        reads=[('XT', xs_), ('WR', ws)], writes=[('ps', b)])
                    P.emit('act', lambda e: e.activation(out=dst[:, 2 * p, :], in_=PS[0:64, b * 512:b * 512 + T], func=AF.Silu),
                           reads=[('ps', b)], writes=[(keyname, 2 * p)])
                    P.emit('act', lambda e: e.activation(out=dst[:, 2 * p + 1, :], in_=PS[64:128, b * 512:b * 512 + T], func=AF.Silu),
                           reads=[('ps', b)], writes=[(keyname, 2 * p + 1)])

              def ev_silu(dst, keyname, M):
                  def f(g, b):
                      P.emit('act', lambda e: e.activation(out=dst[:, g, :], in_=PS[0:M, b * 512:b * 512 + T], func=AF.Silu),
                             reads=[('ps', b)], writes=[(keyname, g)])
                  return f

              def ev_copy(dst, keyname):
                  def f(g, b):
                      P.emit('dve', lambda e: e.tensor_copy(out=dst[:, g, :], in_=PS[:, b * 512:b * 512 + T]),
                             reads=[('ps', b)], writes=[(keyname, g)])
                  return f

              xslot = {}
              xslot[-1] = xt_load(xq, 0)
              xslot[0] = xt_load(xq, T)
              kvb(-1, xslot[-1])
              kvb(0, xslot[0])
              for j in range(nsteps):
                  xslot[j + 1] = xt_load(xq, (j + 2) * T)
                  xc = xslot[j]
                  kvb(j + 1, xslot[j + 1])
                  _phase('proj')
                  rb = j % 2
                  P.dma('sp', RT[:, rb], ropeq[:, j * 2:j * 2 + 2, :], [], [('RT', rb)], ('rt', rb))
                  ws = wload(2)
                  for u in range(2):
                      b = next_bank()
                      P.emit('pe', [lambda e, c=c: e.matmul(
                          PS[:, b * 512:(b + 1) * 512], lhsT=XT[:, xc, c, u * 128:(u + 1) * 128], rhs=WR[:, ws, c * 512:(c + 1) * 512],
                          start=(c == 0), stop=(c == 7)) for c in range(8)],
                          reads=[('XT', xc), ('WR', ws)], writes=[('ps', b)])
                      P.emit('act', lambda e: e.activation(out=XS[:], in_=PS[:, b * 512:(b + 1) * 512], func=AF.Copy),
                             reads=[('ps', b)], writes=[('XS', q) for q in range(NPAR)])
                      rope_chain(8, 'QN', QN, ('RT', rb), RT[:, rb, u, :], [('QR', q) for q in range(NPAR)], QR[:])
                      b2 = next_bank()
                      P.emit('pe', [lambda e, s=s: e.transpose(
                          PS[:, b2 * 512 + s * 64:b2 * 512 + (s + 1) * 64].bitcast(BF16), QR[:, s * 128:(s + 1) * 128], IDENT[:])
                          for s in range(4)], reads=[('QR', q) for q in range(NPAR)] + ['IDENT'], writes=[('ps', b2)])
                      P.emit('dve', lambda e: e.tensor_copy(
                          out=QA[:, u, :, :].rearrange("p s q -> p (s q)"), in_=PS[:, b2 * 512:b2 * 512 + 256].bitcast(BF16)),
                          reads=[('ps', b2)], writes=[('QA', u)])
                  sp_ = j % 2
                  SZA, SZB, SZM = SZA2[sp_], SZB2[sp_], SZM2[sp_]
                  ng_flush()
                  fm_proj_zpair(3, xc, SZA, ('SZA', sp_))
                  fm_proj(4, 4, 128, xc, ev_copy(QB, 'QB'))
                  fm_proj_zpair(5, xc, SZB, ('SZB', sp_))
                  fm_proj(6, 4, 128, xc, ev_copy(QM, 'QM'))
                  fm_proj(7, 4, 128, xc, ev_silu(SZM, ('SZM', sp_), 128))

                  sza_keys = [(('SZA', sp_), g) for g in range(8)]
                  szb_keys = [(('SZB', sp_), g) for g in range(8)]
                  szm_keys = [(('SZM', sp_), g) for g in range(4)]
                  _phase('attB')
                  for sub in range(2):
                      i = 2 * j + sub
                      cls = blk_class(i, nl)
                      offs = OFFS[cls]
                      def b_geom(oidx):
                          kb_ = i + offs[oidx]
                          return (kb_ // 2 + 1) % 3, kb_ % 2

                      def b_qk(oidx):
                          rs, ksub = b_geom(oidx)
                          sb_ = oidx % 2
                          P.emit('pe', [lambda e, h=h: e.matmul(
                              PS[:, sb_ * 1024 + (h % 2) * 512 + (h // 2) * 128:sb_ * 1024 + (h % 2) * 512 + (h // 2 + 1) * 128],
                              lhsT=KB[(h % 2) * 64:(h % 2) * 64 + 64, h // 2, rs, ksub * 128:(ksub + 1) * 128],
                              rhs=QB[(h % 2) * 64:(h % 2) * 64 + 64, h // 2, sub * 128:(sub + 1) * 128], start=True, stop=True)
                              for h in range(8)],
                              reads=[('KB', rs, p) for p in range(4)] + [('QB', p) for p in range(4)],
                              writes=[('ps', 2 * sb_), ('ps', 2 * sb_ + 1)])

                      def b_rest(oidx):
                          rs, ksub = b_geom(oidx)
                          sb_ = oidx % 2
                          o = offs[oidx]
                          p0 = next_pb()
                          p1 = next_pb()
                          P.emit('act', lambda e: e.activation(out=PB[:, p0, :], in_=PS[:, sb_ * 1024:(sb_ + 1) * 1024], func=AF.Exp, scale=0.125),
                                 reads=[('ps', 2 * sb_), ('ps', 2 * sb_ + 1)], writes=[('PB', p0)])
                          oi = o + 3
                          for a in range(2):
                              col = cls * 14 + oi * 2 + a
                              P.emit('dve', lambda e: e.scalar_tensor_tensor(
                                  out=PB[:, p1, :].rearrange("p (h q) -> p h q", h=8)[:, :, a * 64:(a + 1) * 64],
                                  in0=PB[:, p0, :].rearrange("p (h q) -> p h q", h=8)[:, :, a * 64:(a + 1) * 64],
                                  scalar=RV[:, si, col:col + 1], in1=ET[:, oi, :, a * 64:(a + 1) * 64],
                                  op0=ALU.mult, op1=ALU.mult),
                                  reads=[('PB', p0), ('RV', si), 'ET'], writes=[('PB', p1)])
                          P.emit('pe', [lambda e, h=h: e.matmul(
                              PS[0:65, 2048 + h * 128:2048 + (h + 1) * 128], lhsT=VB[:, rs, ksub, h, :],
                              rhs=PB[:, p1, ((h % 2) * 4 + h // 2) * 128:((h % 2) * 4 + h // 2 + 1) * 128],
                              start=(oidx == 0 and h % 4 == 0), stop=(oidx == len(offs) - 1),
                              skip_group_check=True)
                              for h in range(8)],
                              reads=[('VB', rs, ksub), 'VB1', ('PB', p1)], writes=[('ps', 4), ('ps', 5)])
                      b_qk(0)
                      b_qk(1)
                      for oidx in range(len(offs)):
                          if oidx == 2:
                              ng_flush()
                          b_rest(oidx)
                          if oidx + 2 < len(offs):
                              b_qk(oidx + 2)
                      norm_gate(SZB[:, :, sub * 128:(sub + 1) * 128], szb_keys)

                  _phase('attM')
                  ng_flush()
                  for h in range(4):
                      sb_ = h % 2
                      P.emit('pe', [lambda e, mt=mt: e.matmul(
                          PS[:, sb_ * 1024 + mt * T:sb_ * 1024 + (mt + 1) * T], lhsT=KM[:, h, mt * 128:(mt + 1) * 128], rhs=QM[:, h, :],
                          start=True, stop=True) for mt in range(2)],
                          reads=[('KM', h), ('QM', h)], writes=[('ps', 2 * sb_)])
                      pbs = next_pb()
                      P.emit('act', lambda e: e.activation(out=PB[:, pbs, 0:512], in_=PS[:, sb_ * 1024:sb_ * 1024 + 512], func=AF.Exp,
                                                           scale=float(128 ** -0.5)),
                             reads=[('ps', 2 * sb_)], writes=[('PB', pbs)])
                      ob = 4 + h // 2
                      xb = 6 + h // 2
                      oc = (h % 2) * T
                      P.emit('pe', [lambda e, mt=mt: e.matmul(
                          PS[:, ob * 512 + oc:ob * 512 + oc + T], lhsT=VM[:, mt, h * 128:(h + 1) * 128], rhs=PB[:, pbs, mt * T:(mt + 1) * T],
                          start=(mt == 0), stop=(mt == 1)) for mt in range(2)] + [lambda e, mt=mt: e.matmul(
                              PS[:, xb * 512 + oc:xb * 512 + oc + T], lhsT=ONESB[:], rhs=PB[:, pbs, mt * T:(mt + 1) * T],
                              start=(mt == 0), stop=(mt == 1)) for mt in range(2)],
                          reads=[('VM', 0), ('VM', 1), ('PB', pbs), 'ONESB'], writes=[('ps', ob), ('ps', xb)])
                  P.emit('dve', lambda e: e.reciprocal(out=OSB[:], in_=PS[:, 3072:4096]),
                         reads=[('ps', 6), ('ps', 7)], writes=['OSB'])
                  P.emit('dve', lambda e: e.tensor_tensor(out=RZ[:], in0=OSB[:], in1=SZM[:].rearrange("p h t -> p (h t)"), op=ALU.mult),
                         reads=['OSB'] + szm_keys, writes=['RZ'])
                  P.emit('dve', lambda e: e.tensor_tensor(out=SZM[:].rearrange("p h t -> p (h t)"), in0=PS[:, 2048:3072], in1=RZ[:], op=ALU.mult),
                         reads=['RZ', ('ps', 4), ('ps', 5)], writes=szm_keys)

                  _phase('attA')
                  pop_stride = 4 if nkt >= 128 else (3 if nkt >= 48 else 2)
                  for u in range(2):
                      def qk(kt):
                          sb_ = kt % 2
                          P.emit('pe', [
                              lambda e: e.matmul(PS[:, sb_ * 1024:sb_ * 1024 + 512], lhsT=KA[0:64, kt * 128:(kt + 1) * 128],
                                                 rhs=QA[0:64, u, :, :].rearrange("p s q -> p (s q)"), start=True, stop=True),
                              lambda e: e.matmul(PS[:, sb_ * 1024 + 512:sb_ * 1024 + 1024], lhsT=KA[64:128, kt * 128:(kt + 1) * 128],
                                                 rhs=QA[64:128, u, :, :].rearrange("p s q -> p (s q)"), start=True, stop=True)],
                              reads=[('KA', kt), ('QA', u)], writes=[('ps', 2 * sb_), ('ps', 2 * sb_ + 1)])

                      def ex_pv(kt):
                          sb_ = kt % 2
                          pbs = next_pb()
                          P.emit('act', lambda e: e.activation(out=PB[:, pbs, :], in_=PS[:, sb_ * 1024:(sb_ + 1) * 1024], func=AF.Exp, scale=0.125),
                                 reads=[('ps', 2 * sb_), ('ps', 2 * sb_ + 1)], writes=[('PB', pbs)])
                          P.emit('pe', [
                              lambda e: e.matmul(PS[0:65, 2048:2560], lhsT=VA[:, kt, 0, :], rhs=PB[:, pbs, 0:512],
                                                 start=(kt == 0), stop=(kt == nkt - 1)),
                              lambda e: e.matmul(PS[0:65, 2560:3072], lhsT=VA[:, kt, 1, :], rhs=PB[:, pbs, 512:1024],
                                                 start=(kt == 0), stop=(kt == nkt - 1))],
                              reads=[('VA', kt), 'VA1', ('PB', pbs)], writes=[('ps', 4), ('ps', 5)])
                      qk(0)
                      qk(1)
                      for kt in range(nkt):
                          if kt == min(8, nkt - 1):
                              if mid_job[0] and jobs:
                                  pop_job()
                              ng_flush()
                          if kt % pop_stride == 1 and jobs:
                              pop_job()
                          ex_pv(kt)
                          if kt + 2 < nkt:
                              qk(kt + 2)
                      norm_gate(SZA[:, :, u * 128:(u + 1) * 128], sza_keys)

                  _phase('merge')
                  while jobs:
                      pop_job()

                  def build_jobs(j=j, xc=xc, SZA=SZA, SZB=SZB, SZM=SZM, sza_keys=sza_keys, szb_keys=szb_keys,
                                 szm_keys=szm_keys, n=n, ydr=ydr, xtok=xtok):
                      out = []
                      jk = [0]
                      order = [(tt, ct, nb_) for tt in range(2) for ct in range(2) for nb_ in (2, 1, 0)]
                      NO = len(order)
                      allslots = [dict() for _ in range(NO + 1)]
                      MBv = [XS[:].bitcast(BF16), ORO[:].bitcast(BF16)]
                      mbk = [[('XS', q) for q in range(NPAR)], [('ORO', q) for q in range(NPAR)]]

                      def ensure(g_, which):
                          sl = allslots[g_]
                          if which not in sl:
                              if g_ < NO:
                                  _, ct, nb_ = order[g_]
                                  sl[which] = wload(8 + ct * 6 + nb_ * 2 + (0 if which == 'wg' else 1))
                              else:
                                  sl[which] = wload(24 if which == 'wg' else 25)
                      for g_, (tt, ct, nb_) in enumerate(order):
                          slots = allslots[g_]
                          idx = g_ % 3
                          if True:
                              def p1(g_=g_, ct=ct, nb_=nb_, slots=slots, tt=tt):
                                  ensure(g_, 'wg')
                                  ensure(g_, 'wb')
                                  wg = slots['wg']
                                  P.emit('pe', [lambda e, c=c: e.matmul(
                                      PS[:, 3072:3584], lhsT=XT[:, xc, c, tt * 128:(tt + 1) * 128], rhs=WR[:, wg, c * 512:(c + 1) * 512],
                                      start=(c == 0), stop=(c == 7)) for c in range(8)],
                                      reads=[('XT', xc), ('WR', wg)], writes=[('ps', 6)])
                                  ensure(g_ + 1, 'wg')

                              def p2(g_=g_, ct=ct, nb_=nb_, slots=slots, tt=tt, idx=idx):
                                  wb = slots['wb']
                                  k = jk[0] % 2
                                  jk[0] += 1
                                  if nb_ == 0:
                                      P.emit('pe', [lambda e, h=h: e.matmul(
                                          PS[:, 3584:4096], lhsT=SZA[:, h, tt * 128:(tt + 1) * 128], rhs=WR[0:64, wb, h * 512:(h + 1) * 512],
                                          start=(h == 0), stop=(h == 7)) for h in range(8)],
                                          reads=[('WR', wb)] + sza_keys, writes=[('ps', 7)])
                                  elif nb_ == 1:
                                      P.emit('pe', [lambda e, h=h: e.matmul(
                                          PS[:, 3584:4096], lhsT=SZB[:, h, tt * 128:(tt + 1) * 128], rhs=WR[0:64, wb, h * 512:(h + 1) * 512],
                                          start=(h == 0), stop=(h == 7)) for h in range(8)],
                                          reads=[('WR', wb)] + szb_keys, writes=[('ps', 7)])
                                  else:
                                      P.emit('pe', [lambda e, h=h: e.matmul(
                                          PS[:, 3584:4096], lhsT=SZM[:, h, tt * 128:(tt + 1) * 128], rhs=WR[:, wb, h * 512:(h + 1) * 512],
                                          start=(h == 0), stop=(h == 3)) for h in range(4)],
                                          reads=[('WR', wb)] + szm_keys, writes=[('ps', 7)])
                                  ensure(g_ + 1, 'wb')
                                  P.emit('act', lambda e: e.activation(out=SGT[:, k, :], in_=PS[:, 3072:3584], func=AF.Tanh, scale=0.5),
                                         reads=[('ps', 6)], writes=[('SGT', k)])
                                  acc = YB[:, tt, ct * 512:(ct + 1) * 512]
                                  yk = ('YB', tt)
                                  if idx == 0:
                                      P.emit('dve', lambda e: e.scalar_tensor_tensor(
                                          out=acc, in0=SGT[:, k, :], scalar=1.0, in1=PS[:, 3584:4096], op0=ALU.add, op1=ALU.mult),
                                          reads=[('SGT', k), ('ps', 7)], writes=[yk])
                                  else:
                                      P.emit('dve', lambda e: e.scalar_tensor_tensor(
                                          out=MT[:], in0=SGT[:, k, :], scalar=1.0, in1=PS[:, 3584:4096], op0=ALU.add, op1=ALU.mult),
                                          reads=[('SGT', k), ('ps', 7)], writes=['MT'])
                                      if idx == 1:
                                          P.emit('dve', lambda e: e.tensor_tensor(out=acc, in0=acc, in1=MT[:], op=ALU.add),
                                                 reads=[yk, 'MT'], writes=[yk])
                                      else:
                                          P.emit('dve', lambda e: e.tensor_tensor(out=MT[:], in0=acc, in1=MT[:], op=ALU.add),
                                                 reads=[yk, 'MT'], writes=['MT'])
                                          P.emit('dve', lambda e: e.tensor_scalar(out=MBv[tt][:, ct * 512:(ct + 1) * 512], in0=MT[:], scalar1=0.5,
                                                                                  scalar2=None, op0=ALU.mult),
                                                 reads=['MT'], writes=mbk[tt])
                              p1.half = 1
                              p2.half = 2
                              out.append(p1)
                              out.append(p2)
                      for tt in range(2):
                          def tr(tt=tt):
                              P.emit('pe', [lambda e, c=c: e.transpose(
                                  PS[:, 3072 + c * 64:3072 + (c + 1) * 64].bitcast(BF16), MBv[tt][:, c * 128:(c + 1) * 128], IDENT[:])
                                  for c in range(8)], reads=mbk[tt] + ['IDENT'], writes=[('ps', 6)])
                              P.emit('dve', lambda e: e.tensor_copy(
                                  out=MG[:, :, tt * 128:(tt + 1) * 128],
                                  in_=PS[:, 3072:3584].bitcast(BF16).rearrange("p (c q) -> p c q", c=8)),
                                  reads=[('ps', 6)], writes=[('MG', c) for c in range(8)])

                          def o1(tt=tt):
                              wo0 = allslots[NO]['wg']
                              P.emit('pe', [lambda e, c=c: e.matmul(
                                  PS[:, 3072:3584], lhsT=MG[:, c, tt * 128:(tt + 1) * 128], rhs=WR[:, wo0, c * 512:(c + 1) * 512],
                                  start=(c == 0), stop=(c == 7)) for c in range(8)],
                                  reads=[('MG', c) for c in range(8)] + [('WR', wo0)], writes=[('ps', 6)])

                          def o2(tt=tt):
                              wo1 = allslots[NO]['wb']
                              tok0 = j * T + tt * 128
                              P.emit('pe', [lambda e, c=c: e.matmul(
                                  PS[:, 3584:4096], lhsT=MG[:, c, tt * 128:(tt + 1) * 128], rhs=WR[:, wo1, c * 512:(c + 1) * 512],
                                  start=(c == 0), stop=(c == 7)) for c in range(8)],
                                  reads=[('MG', c) for c in range(8)] + [('WR', wo1)], writes=[('ps', 7)])
                              P.dma('sp', XTOK[:], xtok[tok0:tok0 + 128, :], [], ['XTOK'], 'xtok')
                              for ct in range(2):
                                  P.emit('dve', lambda e: e.scalar_tensor_tensor(
                                      out=YB[:, tt, ct * 512:(ct + 1) * 512], in0=XTOK[:, ct * 512:(ct + 1) * 512], scalar=float(DN_ALPHA),
                                      in1=PS[:, 3072 + ct * 512:3072 + (ct + 1) * 512], op0=ALU.mult, op1=ALU.add),
                                      reads=['XTOK', ('ps', 6 + ct)], writes=[('YB', tt)])
                          tr.half = 0
                          o1.half = 1
                          o2.half = 2
                          out.append(tr)
                          out.append(o1)
                          out.append(o2)

                      def ln_tail(part):
                          for tt in range(2):
                              tok0 = j * T + tt * 128
                              yk = ('YB', tt)
                              sk = ('ST', tt)
                              if part == 0:
                                  P.emit('dve', lambda e: e.tensor_reduce(out=ST[:, tt, 0:1], in_=YB[:, tt, :], axis=AX.X, op=ALU.add),
                                         reads=[yk], writes=[sk])
                                  P.emit('dve', lambda e: e.tensor_scalar(out=ST[:, tt, 1:2], in0=ST[:, tt, 0:1], scalar1=-1.0 / D, scalar2=None, op0=ALU.mult),
                                         reads=[sk], writes=[sk])
                                  P.emit('dve', lambda e: e.tensor_scalar(out=YB[:, tt, :], in0=YB[:, tt, :], scalar1=ST[:, tt, 1:2], scalar2=None, op0=ALU.add),
                                         reads=[sk, yk], writes=[yk])
                                  P.emit('pool', lambda e: e.tensor_tensor(out=XTOK[:], in0=YB[:, tt, :], in1=YB[:, tt, :], op=ALU.mult),
                                         reads=[yk], writes=['XTOK'])
                                  P.emit('dve', lambda e: e.tensor_reduce(out=ST[:, tt, 2:3], in_=XTOK[:], axis=AX.X, op=ALU.add),
                                         reads=['XTOK'], writes=[sk])
                              else:
                                  P.emit('act', lambda e: e.activation(out=ST[:, tt, 3:4], in_=ST[:, tt, 2:3], func=AF.Sqrt, bias=EPS[:, 1:2], scale=1.0 / D),
                                         reads=[sk, 'EPS'], writes=[sk])
                                  P.emit('dve', lambda e: e.reciprocal(out=ST[:, tt, 4:5], in_=ST[:, tt, 3:4]), reads=[sk], writes=[sk])
                                  P.emit('dve', lambda e: e.scalar_tensor_tensor(out=YB[:, tt, :], in0=YB[:, tt, :], scalar=ST[:, tt, 4:5], in1=LNG[:],
                                                                                 op0=ALU.mult, op1=ALU.mult),
                                         reads=[sk, yk, 'LNG'], writes=[yk])
                                  P.emit('pool', lambda e: e.tensor_tensor(out=YB[:, tt, :], in0=YB[:, tt, :], in1=LNB[:], op=ALU.add),
                                         reads=[yk, 'LNB'], writes=[yk])
                                  P.dma('pool', ydr[tok0:tok0 + 128, :], YB[:, tt, :], [yk], [('ydr', n, j, tt)], ('yst', tt))
                      out.append(lambda: ln_tail(0))
                      out.append(lambda: None)
                      out.append(lambda: ln_tail(1))
                      return out
                  jobs.extend(build_jobs())
              ng_flush()
              while jobs:
                  pop_job()

        except _StopEmission:
            pass
        P.final_wait('pool')

        sems = {}
        for k in P.cnt:
            sems[k] = es.enter_context(nc.semaphore("s%d" % len(sems)))
        block = es.enter_context(nc.Block())

        def run(name, e):
            for waits, fns, key, inc in P.engs[name]:
                for (k, v) in waits:
                    e.wait_ge(sems[k], v)
                if fns is None:
                    continue
                for f in fns[:-1]:
                    f(e)
                fns[-1](e).then_inc(sems[key], inc)

        @block.tensor
        def _(e):
            run('pe', e)

        @block.scalar
        def _(e):
            run('act', e)

        @block.vector
        def _(e):
            run('dve', e)

        @block.gpsimd
        def _(e):
            run('pool', e)

        @block.sync
        def _(e):
            run('sp', e)
    return nc, P


def _tile_k8(W):
    C = W.shape[1]
    return np.ascontiguousarray(W.reshape(8, 128, C).transpose(1, 0, 2).reshape(128, 8 * C))


def _weight_tiles(w_in, w_mem_kv, w_branch, w_out):
    wt = np.zeros((NT, 128, 4096), np.float32)
    qa_cols = np.concatenate([np.arange(h * 64, (h + 1) * 64) for h in (0, 4, 1, 5, 2, 6, 3, 7)])
    wt[0] = _tile_k8(w_in[:, 1792:2304])
    wt[1] = _tile_k8(w_in[:, 2304:2816])
    wt[2] = _tile_k8(w_in[:, 0:512][:, qa_cols])
    wt[3] = _tile_k8(w_in[:, 768:1280])
    wt[4] = _tile_k8(w_in[:, 1280:1792])
    wt[5] = _tile_k8(w_in[:, 2816:3328])
    wt[6] = _tile_k8(w_in[:, 3328:3840])
    wt[7] = _tile_k8(w_in[:, 3840:4352])
    for ct in range(2):
        for nb in range(3):
            base = 8 + ct * 6 + nb * 2
            wt[base] = _tile_k8(w_in[:, 4352 + nb * 1024 + ct * 512:4352 + nb * 1024 + (ct + 1) * 512])
            wb = w_branch[nb][:, ct * 512:(ct + 1) * 512]
            if nb < 2:
                wt[base + 1, 0:64, :] = wb.reshape(8, 64, 512).transpose(1, 0, 2).reshape(64, 4096)
            else:
                wt[base + 1, :, 0:2048] = wb.reshape(4, 128, 512).transpose(1, 0, 2).reshape(128, 2048)
    wt[24] = _tile_k8(w_out[:, 0:512])
    wt[25] = _tile_k8(w_out[:, 512:1024])
    wt[26, :, 0:2048] = _tile_k8(w_in[:, 512:768])
    wt[27] = _tile_k8(w_mem_kv[:, 0:512])
    wt[28] = _tile_k8(w_mem_kv[:, 512:1024])
    return wt


def _rope_table(pos):
    pos2 = np.stack([pos // GRID_W, pos % GRID_W], axis=-1).astype(np.float32)
    inv_freq = (np.float32(10000.0) ** (-np.arange(16, dtype=np.float32) / np.float32(16))).astype(np.float32)
    ang = (pos2[:, :, None] * inv_freq).astype(np.float32)
    return np.ascontiguousarray(np.concatenate([np.cos(ang).reshape(-1, 32), np.sin(ang).reshape(-1, 32)], axis=1).astype(np.float32))


def _rv_table(S, q0, nq):
    R = S // GRID_W
    nl = nq // 128
    rv = np.zeros((128, 5, 7, 2), np.float32)
    kr_l = np.arange(128) // 64
    for i in range(nl):
        cls = blk_class(i, nl)
        gi = q0 // 128 + i
        for o in OFFS[cls]:
            for a in range(2):
                qr = 2 * gi + a
                q_wr = min(max(qr - 4, 0), R - 8)
                kr = 2 * (gi + o) + kr_l
                ok = (kr >= 0) & (kr < R) & (kr >= q_wr) & (kr < q_wr + 8)
                rv[:, cls, o + 3, a] = ok.astype(np.float32)
    return np.ascontiguousarray(rv.reshape(128, 70))


def _bias_tables(rpb):
    k = np.arange(128)
    q = np.arange(128)
    krl, kc = k // 64, k % 64
    qrl, qc = q // 64, q % 64
    bt = np.zeros((128, 7, 8, 128), np.float32)
    cm = np.zeros((128, 7, 128), np.float32)
    q_wc = np.clip(qc - 8, 0, GRID_W - 16)
    dc = kc[:, None] - qc[None, :] + 15
    col_ok = (kc[:, None] >= q_wc[None, :]) & (kc[:, None] < q_wc[None, :] + 16)
    for oi in range(7):
        o = oi - 3
        drr = 2 * o + krl[:, None] - qrl[None, :] + 7
        ok = col_ok & (drr >= 0) & (drr <= 14) & (dc >= 0) & (dc <= 30)
        cm[:, oi, :] = ok
        bt[:, oi, :, :] = rpb[[0, 2, 4, 6, 1, 3, 5, 7]][:, np.clip(drr, 0, 14), np.clip(dc, 0, 30)].transpose(1, 0, 2)
    return np.ascontiguousarray(bt.reshape(128, 7 * 8 * 128)), np.ascontiguousarray(cm.reshape(128, 7 * 128))


def _xt_tiles(xT):
    ns = xT.shape[1] // T
    return np.ascontiguousarray(xT.reshape(8, 128, ns, T).transpose(2, 1, 0, 3).reshape(ns, 128, 8 * T))


_prog_cache = {}


def _make_in_maps(x_prompt, x_sample, mem_prompt, mem_sample, w_in, q_norm, k_norm, rpb, w_mem_kv,
                  w_branch, w_out, ln_g, ln_b):
    x_prompt = np.asarray(x_prompt, np.float32)
    x_sample = np.asarray(x_sample, np.float32)
    mem_prompt = np.asarray(mem_prompt, np.float32)
    mem_sample = np.asarray(mem_sample, np.float32)
    Bp, Sp, _ = x_prompt.shape
    Bs, Ss, _ = x_sample.shape
    assert Bp == 2 and Bs == 8
    Pq = Sp // 4
    wt = _weight_tiles(np.asarray(w_in[0], np.float32), np.asarray(w_mem_kv[0], np.float32),
                       np.asarray(w_branch[0], np.float32), np.asarray(w_out[0], np.float32))
    biastab, cmask = _bias_tables(np.asarray(rpb[0], np.float32))
    qn = np.ascontiguousarray(np.tile(np.asarray(q_norm[0], np.float32)[None, :], (128, 1)))
    kn = np.ascontiguousarray(np.tile(np.asarray(k_norm[0], np.float32)[None, :], (128, 1)))
    lng = np.ascontiguousarray(np.tile(np.asarray(ln_g[0], np.float32)[None, :], (128, 1)))
    lnb = np.ascontiguousarray(np.tile(np.asarray(ln_b[0], np.float32)[None, :], (128, 1)))
    rope_s = _rope_table(np.arange(Ss))
    rope_p = _rope_table(np.arange(Sp))
    rv_s = _rv_table(Ss, 0, Ss)
    xkv_p = [np.ascontiguousarray(x_prompt[b].T) for b in range(Bp)]
    xkv_t = [_xt_tiles(x) for x in xkv_p]
    memT_p = [np.ascontiguousarray(mem_prompt[b].T) for b in range(Bp)]

    in_maps = []
    for c in range(8):
        m = dict(wt=wt, biastab=biastab, cmask=cmask, qn=qn, kn=kn, lng=lng, lnb=lnb)
        xs = x_sample[c]
        xq = np.zeros((D, Ss + 2 * T), np.float32)
        xq[:, T:T + Ss] = xs.T
        m['s_xq'] = _xt_tiles(xq)
        m['s_xtok'] = np.ascontiguousarray(xs)
        m['s_ropekv'] = rope_s
        m['s_ropeq'] = rope_s
        m['s_rv'] = rv_s
        m['s_memT'] = np.ascontiguousarray(mem_sample[c].T)
        b, qd = c // 4, c % 4
        q0 = qd * Pq
        xq = np.zeros((D, Pq + 2 * T), np.float32)
        lo, hi = max(q0 - T, 0), min(q0 + Pq + T, Sp)
        xq[:, lo - (q0 - T):hi - (q0 - T)] = xkv_p[b][:, lo:hi]
        m['p_xq'] = _xt_tiles(xq)
        m['p_xkv'] = xkv_t[b]
        m['p_xtok'] = np.ascontiguousarray(x_prompt[b, q0:q0 + Pq])
        m['p_ropekv'] = rope_p
        m['p_ropeq'] = np.ascontiguousarray(rope_p[q0:q0 + Pq])
        m['p_rv'] = _rv_table(Sp, q0, Pq)
        m['p_memT'] = memT_p[b]
        in_maps.append(m)
    return in_maps


def kernel(x_prompt, x_sample, mem_prompt, mem_sample, w_in, q_norm, k_norm, rpb, w_mem_kv,
           w_branch, w_out, ln_g, ln_b):
    Bp, Sp, _ = x_prompt.shape
    Bs, Ss, _ = x_sample.shape
    Pq = Sp // 4
    in_maps = _make_in_maps(x_prompt, x_sample, mem_prompt, mem_sample, w_in, q_norm, k_norm, rpb, w_mem_kv,
                            w_branch, w_out, ln_g, ln_b)
    key = (Ss, Sp)
    if key not in _prog_cache:
        _prog_cache[key] = build_program(Ss, Sp)[0]
    nc = _prog_cache[key]
    res = run_bass_kernel_spmd(nc, in_maps, core_ids=list(range(8)))
    y_prompt = np.zeros((Bp, Sp, D), np.float32)
    y_sample = np.zeros((Bs, Ss, D), np.float32)
    for c in range(8):
        r = res.results[c]
        y_sample[c] = r['s_y']
        b, qd = c // 4, c % 4
        y_prompt[b, qd * Pq:(qd + 1) * Pq] = r['p_y']
    return (y_prompt, y_sample)
```
